# Optimizing a Trainium2 kernel written in Bass

```python
import math
import jax, jax.numpy as jnp
from jax import lax
import numpy as np


D_MODEL = 1024
BATCH = 32
SEQ = 2048
DEPTH = 1
DEC_BATCH = 16
DEC_SEQ = 2048
PAST_LEN = 128

N_MEM = 256
EPS = 1e-6
S5_WIDTH = D_MODEL // 2
S5_GROUP = 16
S5_GROUPS = S5_WIDTH // S5_GROUP
S5_STATE = 64
DIFF_HEADS = 4
DIFF_DH = D_MODEL // 16
DIFF_VDH = 2 * DIFF_DH
DIFF_QK_WIDTH = 2 * DIFF_HEADS * DIFF_DH
DIFF_WIDTH = DIFF_HEADS * DIFF_VDH
X_HEADS = 4
X_DH = D_MODEL // 8
X_WIDTH = X_HEADS * X_DH
BRANCH_WIDTH = D_MODEL // 2
N_BRANCH = 3
D_FF = -(-8 * D_MODEL // (3 * 256)) * 256
ROPE_THETA = 500000.0
ROPE_DIM = DIFF_DH // 4
Q_BLOCK = 128
IN_SPLITS = (S5_WIDTH,
             S5_WIDTH + DIFF_QK_WIDTH,
             S5_WIDTH + 2 * DIFF_QK_WIDTH,
             S5_WIDTH + 2 * DIFF_QK_WIDTH + DIFF_WIDTH,
             S5_WIDTH + 2 * DIFF_QK_WIDTH + DIFF_WIDTH + X_WIDTH)
IN_COLS = IN_SPLITS[-1] + N_BRANCH * D_MODEL

kernel_name = 'hybrid_s5_diffattn_memory_encoder'


def _rmsnorm(x, g):
    xf = x.astype(jnp.float32)
    y = xf * lax.rsqrt(jnp.mean(xf * xf, axis=-1, keepdims=True) + EPS)
    return (y * g.astype(jnp.float32)).astype(x.dtype)


def _rope_tables(L):
    inv = 1.0 / (ROPE_THETA ** (jnp.arange(0, ROPE_DIM, 2, dtype=jnp.float32) / ROPE_DIM))
    ang = jnp.arange(L, dtype=jnp.float32)[:, None] * inv[None, :]
    return jnp.cos(ang), jnp.sin(ang)


def _partial_rope(x, cos, sin):
    half = ROPE_DIM // 2
    xr = x[..., :ROPE_DIM].astype(jnp.float32)
    x1, x2 = xr[..., :half], xr[..., half:]
    c = cos[None, :, None, :]
    s = sin[None, :, None, :]
    rot = jnp.concatenate([x1 * c - x2 * s, x2 * c + x1 * s], axis=-1).astype(x.dtype)
    return jnp.concatenate([rot, x[..., ROPE_DIM:]], axis=-1)


def _ssm_combine(e1, e2):
    a1, b1 = e1
    a2, b2 = e2
    return a1 * a2, a2 * b1 + b2


def _s5_scan_dir(u, lam, step, b, c):
    lam_bar = jnp.exp(lam * step[:, None])
    b_bar = ((lam_bar - 1.0) / lam)[..., None] * b
    bu = jnp.einsum('lgh,gph->lgp', u.astype(jnp.complex64), b_bar)
    a = jnp.broadcast_to(lam_bar, bu.shape)
    _, xs = lax.associative_scan(_ssm_combine, (a, bu), axis=0)
    return jnp.real(jnp.einsum('lgp,ghp->lgh', xs, c))


def _s5_branch(u, lam_re, lam_im, log_step, b_re, b_im, c_re, c_im, d, w_glu, b_glu):
    dtype = u.dtype
    Bn, L, _ = u.shape
    f32 = jnp.float32
    lam = lax.complex(lam_re.astype(f32), lam_im.astype(f32))
    step = jnp.exp(log_step.astype(f32))
    bmat = lax.complex(b_re.astype(f32), b_im.astype(f32))
    cmat = lax.complex(c_re.astype(f32), c_im.astype(f32))
    uf = u.astype(f32)
    ug = uf.reshape(Bn, L, S5_GROUPS, S5_GROUP)

    def one_seq(us):
        y_f = _s5_scan_dir(us, lam[0], step[0], bmat[0], cmat[0])
        y_b = jnp.flip(_s5_scan_dir(jnp.flip(us, 0), lam[1], step[1], bmat[1], cmat[1]), 0)
        return y_f + y_b

    y = lax.map(one_seq, ug).reshape(Bn, L, S5_WIDTH) + d.astype(f32) * uf
    z = jax.nn.gelu(y).astype(dtype)
    return z * jax.nn.sigmoid(z @ w_glu + b_glu)


def _diff_attention(q, k, v, lam, subln_g, lambda_init):
    Bn, L = q.shape[0], q.shape[1]
    nb = L // Q_BLOCK
    qb = q.reshape(Bn, nb, Q_BLOCK, 2 * DIFF_HEADS, DIFF_DH).transpose(1, 0, 2, 3, 4)
    scale = DIFF_DH ** -0.5

    def block(qblk):
        s = jnp.einsum('bqhd,bkhd->bhqk', qblk, k).astype(jnp.float32) * scale
        probs = jax.nn.softmax(s, axis=-1).reshape(Bn, DIFF_HEADS, 2, Q_BLOCK, L)
        w = probs[:, :, 0] - lam * probs[:, :, 1]
        return jnp.einsum('bhqk,bkhe->bqhe', w.astype(v.dtype), v)

    o = lax.map(block, qb)
    o = o.transpose(1, 0, 2, 3, 4).reshape(Bn, L, DIFF_HEADS, DIFF_VDH)
    o = _rmsnorm(o, subln_g) * (1.0 - lambda_init)
    return o.reshape(Bn, L, DIFF_WIDTH)


def _cross_attention(q, mk, mv):
    Bn, L = q.shape[0], q.shape[1]
    s = jnp.einsum('blhd,bmhd->bhlm', q, mk).astype(jnp.float32) * (X_DH ** -0.5)
    probs = jax.nn.softmax(s, axis=-1).astype(mv.dtype)
    return jnp.einsum('bhlm,bmhd->blhd', probs, mv).reshape(Bn, L, X_WIDTH)


def _layer(x, mem, l, p):
    Bn, L, _ = x.shape
    xn = _rmsnorm(x, p['norm_mix'][l])
    mn = _rmsnorm(mem, p['norm_mem'][l])
    h = xn @ p['w_in'][l]
    u, q, k, v, xq, g = jnp.split(h, IN_SPLITS, axis=-1)
    y_s5 = _s5_branch(u, p['s5_lambda_re'][l], p['s5_lambda_im'][l], p['s5_log_step'][l],
                      p['s5_b_re'][l], p['s5_b_im'][l], p['s5_c_re'][l], p['s5_c_im'][l],
                      p['s5_d'][l], p['s5_w_glu'][l], p['s5_b_glu'][l])
    cos, sin = _rope_tables(L)
    q = _partial_rope(q.reshape(Bn, L, 2 * DIFF_HEADS, DIFF_DH), cos, sin)
    k = _partial_rope(k.reshape(Bn, L, 2 * DIFF_HEADS, DIFF_DH), cos, sin)
    v = v.reshape(Bn, L, DIFF_HEADS, DIFF_VDH)
    lambda_init = 0.8 - 0.6 * math.exp(-0.3 * l)
    f32 = jnp.float32
    lam = (jnp.exp(jnp.sum(p['diff_lambda_q1'][l].astype(f32) * p['diff_lambda_k1'][l].astype(f32)))
           - jnp.exp(jnp.sum(p['diff_lambda_q2'][l].astype(f32) * p['diff_lambda_k2'][l].astype(f32)))
           + lambda_init)
    y_diff = _diff_attention(q, k, v, lam, p['diff_subln'][l], lambda_init)
    mk, mv = jnp.split(mn @ p['w_mem_kv'][l], 2, axis=-1)
    mk = mk.reshape(Bn, N_MEM, X_HEADS, X_DH)
    mv = mv.reshape(Bn, N_MEM, X_HEADS, X_DH)
    y_x = _cross_attention(xq.reshape(Bn, L, X_HEADS, X_DH), mk, mv)
    branches = (y_s5, y_diff, y_x)
    merged = None
    for n in range(N_BRANCH):
        term = jax.nn.sigmoid(g[..., n * D_MODEL:(n + 1) * D_MODEL]) * (branches[n] @ p['w_up'][l, n])
        merged = term if merged is None else merged + term
    x = x + merged @ p['w_out'][l]
    xn2 = _rmsnorm(x, p['norm_ffn'][l])
    gate, up = jnp.split(xn2 @ p['w_ffn_in'][l], 2, axis=-1)
    return x + (jax.nn.silu(gate) * up) @ p['w_ffn_out'][l]


def _trunk(x, mem, p):
    for l in range(DEPTH):
        x = _layer(x, mem, l, p)
    return _rmsnorm(x, p['norm_final'])


def setup_inputs(seed: int = 0) -> dict:
    key = jax.random.key(seed)
    ks = jax.random.split(key, 32)
    f32 = jnp.float32

    def nrm(k, shape, scale):
        return jax.random.normal(k, shape, f32) * scale

    n_idx = jnp.arange(S5_STATE, dtype=f32)
    return {
        'x_prompt': nrm(ks[0], (BATCH, SEQ, D_MODEL), 1.0),
        'x_sample': nrm(ks[1], (DEC_BATCH, DEC_SEQ, D_MODEL), 1.0),
        'mem_prompt': nrm(ks[2], (BATCH, N_MEM, D_MODEL), 1.0),
        'mem_sample': nrm(ks[3], (DEC_BATCH, N_MEM, D_MODEL), 1.0),
        'norm_mix': 1.0 + nrm(ks[4], (DEPTH, D_MODEL), 0.02),
        'norm_mem': 1.0 + nrm(ks[5], (DEPTH, D_MODEL), 0.02),
        'w_in': nrm(ks[6], (DEPTH, D_MODEL, IN_COLS), D_MODEL ** -0.5),
        's5_lambda_re': -0.5 + nrm(ks[7], (DEPTH, 2, S5_GROUPS, S5_STATE), 0.01),
        's5_lambda_im': math.pi * n_idx + nrm(ks[8], (DEPTH, 2, S5_GROUPS, S5_STATE), 0.01),
        's5_log_step': jax.random.uniform(ks[9], (DEPTH, 2, S5_GROUPS), f32,
                                          minval=math.log(1e-3), maxval=math.log(1e-1)),
        's5_b_re': nrm(ks[10], (DEPTH, 2, S5_GROUPS, S5_STATE, S5_GROUP), (2 * S5_GROUP) ** -0.5),
        's5_b_im': nrm(ks[11], (DEPTH, 2, S5_GROUPS, S5_STATE, S5_GROUP), (2 * S5_GROUP) ** -0.5),
        's5_c_re': nrm(ks[12], (DEPTH, 2, S5_GROUPS, S5_GROUP, S5_STATE), (2 * S5_STATE) ** -0.5),
        's5_c_im': nrm(ks[13], (DEPTH, 2, S5_GROUPS, S5_GROUP, S5_STATE), (2 * S5_STATE) ** -0.5),
        's5_d': nrm(ks[14], (DEPTH, S5_WIDTH), 1.0),
        's5_w_glu': nrm(ks[15], (DEPTH, S5_WIDTH, S5_WIDTH), S5_WIDTH ** -0.5),
        's5_b_glu': nrm(ks[16], (DEPTH, S5_WIDTH), 0.01),
        'diff_lambda_q1': nrm(ks[17], (DEPTH, DIFF_DH), 0.1),
        'diff_lambda_k1': nrm(ks[18], (DEPTH, DIFF_DH), 0.1),
        'diff_lambda_q2': nrm(ks[19], (DEPTH, DIFF_DH), 0.1),
        'diff_lambda_k2': nrm(ks[20], (DEPTH, DIFF_DH), 0.1),
        'diff_subln': 1.0 + nrm(ks[21], (DEPTH, DIFF_VDH), 0.02),
        'w_mem_kv': nrm(ks[22], (DEPTH, D_MODEL, 2 * X_WIDTH), D_MODEL ** -0.5),
        'w_up': nrm(ks[23], (DEPTH, N_BRANCH, BRANCH_WIDTH, D_MODEL), BRANCH_WIDTH ** -0.5),
        'w_out': nrm(ks[24], (DEPTH, D_MODEL, D_MODEL), D_MODEL ** -0.5),
        'norm_ffn': 1.0 + nrm(ks[25], (DEPTH, D_MODEL), 0.02),
        'w_ffn_in': nrm(ks[26], (DEPTH, D_MODEL, 2 * D_FF), D_MODEL ** -0.5),
        'w_ffn_out': nrm(ks[27], (DEPTH, D_FF, D_MODEL), D_FF ** -0.5),
        'norm_final': 1.0 + nrm(ks[28], (D_MODEL,), 0.02),
    }


def reference(x_prompt, x_sample, mem_prompt, mem_sample, norm_mix, norm_mem, w_in,
              s5_lambda_re, s5_lambda_im, s5_log_step, s5_b_re, s5_b_im, s5_c_re, s5_c_im,
              s5_d, s5_w_glu, s5_b_glu, diff_lambda_q1, diff_lambda_k1, diff_lambda_q2,
              diff_lambda_k2, diff_subln, w_mem_kv, w_up, w_out, norm_ffn, w_ffn_in,
              w_ffn_out, norm_final):
    p = dict(norm_mix=norm_mix, norm_mem=norm_mem, w_in=w_in,
             s5_lambda_re=s5_lambda_re, s5_lambda_im=s5_lambda_im, s5_log_step=s5_log_step,
             s5_b_re=s5_b_re, s5_b_im=s5_b_im, s5_c_re=s5_c_re, s5_c_im=s5_c_im,
             s5_d=s5_d, s5_w_glu=s5_w_glu, s5_b_glu=s5_b_glu,
             diff_lambda_q1=diff_lambda_q1, diff_lambda_k1=diff_lambda_k1,
             diff_lambda_q2=diff_lambda_q2, diff_lambda_k2=diff_lambda_k2,
             diff_subln=diff_subln, w_mem_kv=w_mem_kv, w_up=w_up, w_out=w_out,
             norm_ffn=norm_ffn, w_ffn_in=w_ffn_in, w_ffn_out=w_ffn_out,
             norm_final=norm_final)
    y_prompt = _trunk(x_prompt, mem_prompt, p)
    y_sample = _trunk(x_sample, mem_sample, p)
    return (y_prompt, y_sample)
```

```python
import math
from contextlib import ExitStack

import numpy as np
import concourse.bass as bass
import concourse.mybir as mybir
from concourse.bass_utils import run_bass_kernel_spmd
from concourse.alu_op_type import AluOpType as ALU

F32 = mybir.dt.float32
BF16 = mybir.dt.bfloat16
I32 = mybir.dt.int32
AF = mybir.ActivationFunctionType

D = 1024
L = 2048
NT = L // 128
NMEM = 256
EPS = 1e-6
INC = 5632
DFF = 2816
ROPE_THETA = 500000.0
LAMBDA_INIT = 0.8 - 0.6 * math.exp(-0.3 * 0)
N_CORES = 8
TB = 512
STOP_AT = 0
K128 = True


class Reg:
    __slots__ = ("w", "rs")

    def __init__(self):
        self.w = {}
        self.rs = {}


class Prog:
    ENG = ("pe", "act", "dve", "pool", "sp")

    def __init__(self, nc, es, n_dma_sems=16):
        self.nc = nc
        self.ops = {k: [] for k in self.ENG}
        self.cnt = {k: 0 for k in self.ENG}
        self.sem = {k: es.enter_context(nc.semaphore("s_" + k)) for k in self.ENG}
        self.R = n_dma_sems
        self.dsem = {q: [es.enter_context(nc.semaphore(f"d_{q}{i}")) for i in range(n_dma_sems)]
                     for q in ("sp", "pool")}
        self.dcnt = {"sp": 0, "pool": 0}
        self.dtok = {"sp": [], "pool": []}
        self.semobj = {}
        for k in self.ENG:
            self.semobj[("e", k)] = self.sem[k]
        for q in self.dsem:
            for i, s in enumerate(self.dsem[q]):
                self.semobj[("d", q, i)] = s
        self.final = []
        self.bar = {k: set() for k in self.ENG}

    def _deps(self, reads, writes, pwrites):
        deps = set()
        for r in reads:
            for k, v in r.w.items():
                deps.add((k, v))
        for r in writes:
            for k, v in r.w.items():
                deps.add((k, v))
            for k, v in r.rs.items():
                deps.add((k, v))
        for r in pwrites:
            for k, v in r.rs.items():
                deps.add((k, v))
        return deps

    def _commit(self, tok, reads, writes, pwrites):
        k, v = tok
        for r in reads:
            if r.rs.get(k, 0) < v:
                r.rs[k] = v
        for r in writes:
            r.w = {k: v}
            r.rs = {}
        for r in pwrites:
            if r.w.get(k, 0) < v:
                r.w[k] = v

    def op(self, eng, fn, reads=(), writes=(), pwrites=(), deps=()):
        deps = self._deps(reads, writes, pwrites) | set(deps) | self.bar[eng]
        self.bar[eng] = set()
        self.cnt[eng] += 1
        tok = (("e", eng), self.cnt[eng])
        self.ops[eng].append((deps, fn, self.sem[eng], 1))
        self._commit(tok, reads, writes, pwrites)
        return tok

    def barrier(self):
        toks = set()
        for k in self.ENG:
            if self.cnt[k] > 0:
                toks.add((("e", k), self.cnt[k]))
        for q in self.dtok:
            for t in self.dtok[q][-self.R:]:
                toks.add(t)
        self.bar = {k: set(toks) for k in self.ENG}

    def dma(self, q, out, in_, reads=(), writes=(), pwrites=(), final=False, **kw):
        deps = self._deps(reads, writes, pwrites) | self.bar[q]
        self.bar[q] = set()
        i = self.dcnt[q]
        self.dcnt[q] += 1
        slot = i % self.R
        val = 16 * (i // self.R + 1)
        if i >= self.R:
            deps.add(self.dtok[q][i - self.R])
        tok = (("d", q, slot), val)
        self.dtok[q].append(tok)
        fn = lambda e, out=out, in_=in_, kw=kw: e.dma_start(out=out, in_=in_, **kw)
        self.ops[q].append((deps, fn, self.dsem[q][slot], 16))
        self._commit(tok, reads, writes, pwrites)
        if final:
            self.final.append(tok)
        return tok

    def emit(self):
        nc = self.nc
        with nc.Block() as block:
            def run(engname):
                def body(e):
                    waited = {}
                    for deps, fn, sem, amt in self.ops[engname]:
                        for (k, v) in sorted(deps, key=lambda t: str(t)):
                            if engname == "pe" and k == ("e", "pe"):
                                continue
                            if waited.get(k, 0) >= v:
                                continue
                            e.wait_ge(self.semobj[k], v)
                            waited[k] = v
                        fn(e).then_inc(sem, amt)
                    if engname == "sp":
                        for (k, v) in self.final:
                            if waited.get(k, 0) < v:
                                e.wait_ge(self.semobj[k], v)
                                waited[k] = v
                return body
            block.sync(run("sp"))
            block.tensor(run("pe"))
            block.scalar(run("act"))
            block.vector(run("dve"))
            block.gpsimd(run("pool"))


def build_nc(NSEQ, stages="ABC", debug=False):
    nc = bass.Bass("TRN2", target_bir_lowering=False)
    T = NSEQ * L

    def din(name, shape):
        return nc.dram_tensor(name, list(shape), F32, kind="ExternalInput").ap()

    x_d = din("x", [T, D])
    mem_d = din("mem", [NSEQ * NMEM, D])
    g_mix = din("norm_mix", [1, D])
    g_mem = din("norm_mem", [1, D])
    g_ffn = din("norm_ffn", [1, D])
    g_fin = din("norm_final", [1, D])
    w_in = din("w_in", [D, INC])
    lam_re = din("s5_lambda_re", [1, 2, 32, 64])
    lam_im = din("s5_lambda_im", [1, 2, 32, 64])
    log_step = din("s5_log_step", [1, 2, 32])
    b_re = din("s5_b_re", [1, 2, 32, 64, 16])
    b_im = din("s5_b_im", [1, 2, 32, 64, 16])
    c_re = din("s5_c_re", [1, 2, 32, 16, 64])
    c_im = din("s5_c_im", [1, 2, 32, 16, 64])
    s5_d = din("s5_d", [1, 512])
    w_glu = din("s5_w_glu", [512, 512])
    b_glu = din("s5_b_glu", [1, 512])
    lq1 = din("diff_lambda_q1", [1, 64])
    lk1 = din("diff_lambda_k1", [1, 64])
    lq2 = din("diff_lambda_q2", [1, 64])
    lk2 = din("diff_lambda_k2", [1, 64])
    subln = din("diff_subln", [1, 128])
    w_kv = din("w_mem_kv", [D, D])
    w_up = din("w_up", [3 * 512, D])
    w_out = din("w_out", [D, D])
    w_f1 = din("w_ffn_in", [D, 2 * DFF])
    w_f2 = din("w_ffn_out", [DFF, D])
    cmask = din("cmask", [128, 4])
    y_d = nc.dram_tensor("y", [T, D], F32, kind="ExternalOutput").ap()

    def dscr(name, shape, dt):
        kind = "ExternalOutput" if debug else "Internal"
        return nc.dram_tensor(name, list(shape), dt, kind=kind).ap()

    wb_in = nc.dram_tensor("wb_in", [D, INC], BF16, kind="Internal").ap()
    wb_kv = nc.dram_tensor("wb_kv", [D, D], BF16, kind="Internal").ap()
    wb_up = nc.dram_tensor("wb_up", [3 * 512, D], BF16, kind="Internal").ap()
    wb_out = nc.dram_tensor("wb_out", [D, D], BF16, kind="Internal").ap()
    wb_f1 = nc.dram_tensor("wb_f1", [D, 2 * DFF], BF16, kind="Internal").ap()
    wb_f2 = nc.dram_tensor("wb_f2", [DFF, D], BF16, kind="Internal").ap()
    wb_glu = nc.dram_tensor("wb_glu", [512, 512], BF16, kind="Internal").ap()
    ct2_d = nc.dram_tensor("ct2_scr", [3, 128, 2048], BF16, kind="Internal").ap()
    bb_d = nc.dram_tensor("bb_scr", [2, 128, 2048], BF16, kind="Internal").ap()
    ys5_d = dscr("ys5_scr", [NSEQ * 512, L], BF16)
    x1_d = dscr("x1_scr", [T, D], F32)

    with ExitStack() as es:
        P = Prog(nc, es)

        def sb(name, shape, dt=F32):
            return es.enter_context(nc.sbuf_tensor(name, list(shape), dt))

        banks = [es.enter_context(nc.psum_tensor(f"pb{i}", [128, 512], F32)) for i in range(6)]
        rbank = [Reg() for _ in range(6)]
        tbanks = [es.enter_context(nc.psum_tensor(f"tb{i}", [128, 1024], BF16)) for i in range(2)]
        rtb = [Reg(), Reg()]
        ring = [0]

        def nb():
            i = ring[0] % 6
            ring[0] += 1
            return i

        tring = [0]

        def ntb():
            i = tring[0] % 2
            tring[0] += 1
            return i

        ident = sb("ident", [128, 128], BF16)
        identf = sb("identf", [128, 128], F32)
        ones = sb("ones", [128, 128], BF16)
        r_const = Reg()
        P.op("pool", lambda e: e.iota(identf[:], pattern=[[1, 128]], base=0, channel_multiplier=-1,
                                      allow_small_or_imprecise_dtypes=True), writes=[r_const])
        P.op("dve", lambda e: e.tensor_scalar(out=ident[:], in0=identf[:], scalar1=0.0, scalar2=None,
                                              op0=ALU.is_equal), reads=[r_const], writes=[r_const])
        P.op("dve", lambda e: e.memset(ones[:], 1.0), writes=[r_const])

        gains = sb("gains", [128, 4, D], BF16)
        r_gain = Reg()

        r_w = {}

        def cast_w(name, src, dst, rows, cols):
            cstep = 1408 if cols % 1408 == 0 else (1024 if cols % 1024 == 0 else cols)
            chunks = []
            for c0 in range(0, cols, cstep):
                r = Reg()
                chunks.append((c0, c0 + cstep, r))
                for r0 in range(0, rows, 128):
                    P.dma("pool", dst[r0:r0 + 128, c0:c0 + cstep], src[r0:r0 + 128, c0:c0 + cstep], pwrites=[r])
            r_w[name] = chunks

        def wregs(name, c0, ncols):
            return [r for (a, b, r) in r_w[name] if a < c0 + ncols and b > c0]

        if "A" in stages:
            cast_w("glu", w_glu, wb_glu, 512, 512)
        cast_w("in", w_in, wb_in, D, INC)
        if "B" in stages:
            cast_w("kv", w_kv, wb_kv, D, D)
            cast_w("up", w_up, wb_up, 3 * 512, D)
            cast_w("out", w_out, wb_out, D, D)
        if "C" in stages:
            cast_w("f1", w_f1, wb_f1, D, 2 * DFF)
            cast_w("f2", w_f2, wb_f2, DFF, D)

        def mk_wpool(slot_aps):
            regs = [Reg() for _ in slot_aps]
            ringc = [0]

            def wload(name, wb, row0, nkc, c0, ncols=512):
                i = ringc[0] % len(slot_aps)
                ringc[0] += 1
                src = wb[row0:row0 + nkc * 128, c0:c0 + ncols].rearrange("(kc p) c -> p kc c", p=128)
                P.dma("sp", slot_aps[i][:, 0:nkc, 0:ncols], src, reads=wregs(name, c0, ncols), writes=[regs[i]])
                return slot_aps[i], regs[i]
            return wload

        base_slots = [sb(f"ws{i}", [128, 8, 512], BF16)[:] for i in range(2)]
        wload = mk_wpool(base_slots)

        xin = [sb(f"xin{i}", [128, D], F32) for i in range(2)]
        r_xin = [Reg(), Reg()]
        xnb = [sb(f"xnb{i}", [128, D], BF16) for i in range(2)]
        r_xnb = [Reg(), Reg()]
        stat = sb("stat", [128, 8], F32)
        r_stat = Reg()
        junk, r_junk = None, None
        xring = [0]
        for i, g in enumerate((g_mix, g_mem, g_ffn, g_fin)):
            P.dma("sp", xin[0][:].rearrange("p (o d) -> p o d", o=1), g.partition_broadcast(128), writes=[r_xin[0]])
            P.op("dve", lambda e, i=i: e.tensor_copy(out=gains[:, i, :], in_=xin[0][:]), reads=[r_xin[0]], pwrites=[r_gain])


        def rms_rows(src_tile, r_src, gain_idx, out_bf, r_out):
            P.op("act", lambda e: e.activation(out=out_bf[:], in_=src_tile[:], func=AF.Square,
                                               accum_out=stat[:, 0:1]), reads=[r_src], writes=[r_out, r_stat])
            P.op("act", lambda e: e.activation(out=stat[:, 1:2], in_=stat[:, 0:1], func=AF.Sqrt,
                                               scale=1.0 / D, bias=EPS), reads=[r_stat], writes=[r_stat])
            P.op("dve", lambda e: e.reciprocal(out=stat[:, 2:3], in_=stat[:, 1:2]), reads=[r_stat], writes=[r_stat])
            P.op("dve", lambda e: e.scalar_tensor_tensor(out=out_bf[:], in0=src_tile[:], scalar=stat[:, 2:3],
                                                         in1=gains[:, gain_idx, :], op0=ALU.mult, op1=ALU.mult),
                 reads=[r_src, r_stat, r_gain], writes=[r_out])

        def transpose_rows(src_bf, r_src, ncol, dst_fn, r_dst, evac="dve", extra_w=()):
            nch = ncol // 128
            for c0 in range(0, nch, 8):
                n = min(8, nch - c0)
                bi = ntb()
                tbk = tbanks[bi]
                for j in range(n):
                    P.op("pe", lambda e, j=j, c0=c0, tbk=tbk: e.transpose(
                        tbk[:, j * 128:(j + 1) * 128],
                        src_bf[:, (c0 + j) * 128:(c0 + j + 1) * 128], ident[:]),
                        reads=[r_src, r_const], writes=([rtb[bi]] if j == 0 else []), pwrites=([] if j == 0 else [rtb[bi]]))
                dsts = dst_fn(c0, n)
                if not isinstance(dsts, list):
                    dsts = [(slice(0, 128), dsts)]
                for psl, dst in dsts:
                    src_ps = tbk[psl, 0:n * 128].rearrange("p (n c) -> p n c", c=128)
                    if evac == "dve":
                        P.op("dve", lambda e, dst=dst, src_ps=src_ps: e.tensor_copy(out=dst, in_=src_ps),
                             reads=[rtb[bi]], pwrites=[r_dst] + list(extra_w))
                    else:
                        P.op("act", lambda e, dst=dst, src_ps=src_ps: e.activation(out=dst, in_=src_ps, func=AF.Copy),
                             reads=[rtb[bi]], pwrites=[r_dst] + list(extra_w))

        def load_norm_T(row0, gain_idx, dstT, r_dstT, tok0, src_d=None, extra_w=()):
            src_d = x_d if src_d is None else src_d
            i = xring[0] % 2
            xring[0] += 1
            P.dma("sp", xin[i][:], src_d[row0:row0 + 128, :], writes=[r_xin[i]])
            rms_rows(xin[i], r_xin[i], gain_idx, xnb[i], r_xnb[i])
            transpose_rows(xnb[i], r_xnb[i], D, lambda c0, n: dstT[:, c0:c0 + n, tok0:tok0 + 128], r_dstT, extra_w=extra_w)
            return i

        shared = {}
        if "B" in stages:
            build_stage_b(locals())
        if "C" in stages:
            build_stage_c(locals())
        if "A" in stages:
            build_stage_a(locals())
        else:
            for seq in range(NSEQ):
                if "B" in stages:
                    shared["run_b_seq"](seq, False)
                if "C" in stages:
                    shared["run_c_seq"](seq)
        P.emit()
    return nc


def build_stage_c(env):
    P, nc, sb = env["P"], env["nc"], env["sb"]
    NSEQ, stages = env["NSEQ"], env["stages"]
    banks, rbank, nb = env["banks"], env["rbank"], env["nb"]
    wload, wb_f1, wb_f2 = env["wload"], env["wb_f1"], env["wb_f2"]
    x1_d, y_d, x_d = env["x1_d"], env["y_d"], env["x_d"]
    gains, r_gain, stat, r_stat, junk, r_junk = (env[k] for k in ("gains", "r_gain", "stat", "r_stat", "junk", "r_junk"))
    rms_rows, transpose_rows = env["rms_rows"], env["transpose_rows"]
    src_d = x1_d if "B" in stages else x_d
    NTT = TB // 128
    sh_ = env["shared"].get("c", None)
    if sh_ is None:
        x1t = [sb(f"c_x1_{i}", [128, D], F32) for i in range(NTT)]
        r_x1t = [Reg() for _ in range(NTT)]
        xn2T = sb("c_xn2T", [128, 8, TB], BF16)
        r_xn2T = Reg()
        sg = [sb(f"c_sg{i}", [128, TB], F32) for i in range(2)]
        r_sg = [Reg(), Reg()]
    else:
        x1t, r_x1t, xn2T, r_xn2T, sg, r_sg = (sh_[k] for k in ("x1t", "r_x1t", "xn2T", "r_xn2T", "sg", "r_sg"))
    xn2b, r_xn2b = env["xnb"][0], env["r_xnb"][0]
    aT = sb("c_aT", [128, DFF // 128, TB], BF16)
    r_aT = Reg()
    env["shared"].update(aT=aT, r_aT=r_aT)
    yt, r_yt = env["xin"], env["r_xin"]
    cst = {"sgi": 0, "yi": 0}
    SHc = env["shared"]
    if "xnT" in SHc:
        xa = SHc["xnT"]
        wload = env["mk_wpool"]([xa[:, :, 512 * i:512 * (i + 1)] for i in range(4)] + env["base_slots"])

    x1sets = [(x1t, r_x1t)]
    xnsets = [(xn2T, r_xn2T)]
    if "vflat" in SHc:
        vf32 = SHc["vflat"][:].bitcast(F32)
        x1sets.append(([vf32[:, i * D:(i + 1) * D] for i in range(NTT)], [Reg() for _ in range(NTT)]))
        kflat = SHc["kT"][:].rearrange("p a b -> p (a b)")
        xnsets.append((kflat[:, 0:8 * TB].rearrange("p (k t) -> p k t", t=TB), Reg()))
    else:
        x1sets.append(x1sets[0])
        xnsets.append(xnsets[0])

    def run_seq(seq):
        P.barrier()
        blks = list(range(seq * (L // TB), (seq + 1) * (L // TB)))
        head(blks[0])
        for i, blk in enumerate(blks):
            ffn_in(blk)
            if i + 1 < len(blks):
                head(blks[i + 1])
            ffn_out(blk)

    env["shared"]["run_c_seq"] = run_seq

    def head(blk):
        x1t_, r_x1t_ = x1sets[blk % 2]
        xn2T_, r_xn2T_ = xnsets[blk % 2]
        r_x1d = env["shared"].get("r_x1d", None)
        t0 = blk * TB
        for tt in range(NTT):
            rd = [r_x1d[t0 // L]] if r_x1d is not None else []
            P.dma("sp", x1t_[tt][:], src_d[t0 + tt * 128: t0 + (tt + 1) * 128, :], reads=rd, writes=[r_x1t_[tt]])
            rms_rows(x1t_[tt], r_x1t_[tt], 2, xn2b, r_xn2b)
            transpose_rows(xn2b, r_xn2b, D, lambda c0, n, tt=tt: xn2T_[:, c0:c0 + n, tt * 128:(tt + 1) * 128], r_xn2T_)

    def ffn_in(blk):
        xn2T_, r_xn2T_ = xnsets[blk % 2]
        sgi = cst["sgi"]
        for j4 in range(0, DFF // 128, 4):
            nj = min(4, DFF // 128 - j4)
            wg, rwg = wload("f1", wb_f1, 0, 8, j4 * 128, nj * 128)
            wu, rwu = wload("f1", wb_f1, 0, 8, DFF + j4 * 128, nj * 128)
            for jj in range(nj):
                j = j4 + jj
                bg, bu = nb(), nb()
                for kc in range(8):
                    P.op("pe", lambda e, kc=kc, jj=jj, bg=bg, wg=wg: e.matmul(
                        banks[bg][:, 0:TB], lhsT=wg[:, kc, jj * 128:(jj + 1) * 128], rhs=xn2T_[:, kc, :],
                        start=(kc == 0), stop=(kc == 7)), reads=[rwg, r_xn2T_], writes=[rbank[bg]])
                for kc in range(8):
                    P.op("pe", lambda e, kc=kc, jj=jj, bu=bu, wu=wu: e.matmul(
                        banks[bu][:, 0:TB], lhsT=wu[:, kc, jj * 128:(jj + 1) * 128], rhs=xn2T_[:, kc, :],
                        start=(kc == 0), stop=(kc == 7)), reads=[rwu, r_xn2T_], writes=[rbank[bu]])
                s = sgi % 2
                sgi += 1
                P.op("act", lambda e, s=s, bg=bg: e.activation(out=sg[s][:], in_=banks[bg][:, 0:TB], func=AF.Silu),
                     reads=[rbank[bg]], writes=[r_sg[s]])
                P.op("dve", lambda e, s=s, bu=bu, j=j: e.tensor_tensor(out=aT[:, j, :], in0=banks[bu][:, 0:TB],
                                                                        in1=sg[s][:], op=ALU.mult),
                     reads=[rbank[bu], r_sg[s]], pwrites=[r_aT])
        cst["sgi"] = sgi

    def ffn_out(blk):
        x1t_, r_x1t_ = x1sets[blk % 2]
        yi = cst["yi"]
        t0 = blk * TB
        kgroups = [(0, 8), (8, 8), (16, 6)]
        for half in range(2):
            bos = [nb() for _ in range(NTT)]
            for gi, (k0, nk) in enumerate(kgroups):
                wt, rwt = wload("f2", wb_f2, k0 * 128, nk, half * 512, 512)
                for tt in range(NTT):
                    bo = bos[tt]
                    for kk in range(nk):
                        first = (gi == 0 and kk == 0)
                        last = (gi == 2 and kk == nk - 1)
                        P.op("pe", lambda e, kk=kk, k0=k0, wt=wt, bo=bo, tt=tt, first=first, last=last: e.matmul(
                            banks[bo][:, :], lhsT=aT[:, k0 + kk, tt * 128:(tt + 1) * 128], rhs=wt[:, kk, :],
                            start=first, stop=last), reads=[rwt, r_aT], writes=[rbank[bo]])
            for tt in range(NTT):
                bo = bos[tt]
                P.op("dve", lambda e, tt=tt, bo=bo, half=half: e.tensor_tensor(
                    out=x1t_[tt][:, half * 512:(half + 1) * 512], in0=banks[bo][:, :],
                    in1=x1t_[tt][:, half * 512:(half + 1) * 512], op=ALU.add),
                    reads=[rbank[bo], r_x1t_[tt]], writes=[r_x1t_[tt]])
        for tt in range(NTT):
            y = yi % 2
            yi += 1
            P.op("act", lambda e, tt=tt, y=y: e.activation(out=yt[y][:], in_=x1t_[tt][:], func=AF.Square,
                                                      accum_out=stat[:, 4:5]), reads=[r_x1t_[tt]], writes=[r_yt[y], r_stat])
            P.op("act", lambda e: e.activation(out=stat[:, 5:6], in_=stat[:, 4:5], func=AF.Sqrt,
                                               scale=1.0 / D, bias=EPS), reads=[r_stat], writes=[r_stat])
            P.op("dve", lambda e: e.reciprocal(out=stat[:, 6:7], in_=stat[:, 5:6]), reads=[r_stat], writes=[r_stat])
            P.op("dve", lambda e, tt=tt, y=y: e.scalar_tensor_tensor(
                out=yt[y][:], in0=x1t_[tt][:], scalar=stat[:, 6:7], in1=gains[:, 3, :], op0=ALU.mult, op1=ALU.mult),
                reads=[r_x1t_[tt], r_stat, r_gain], writes=[r_yt[y]])
            P.dma("pool", y_d[t0 + tt * 128: t0 + (tt + 1) * 128, :], yt[y][:], reads=[r_yt[y]], final=True)
        cst["yi"] = yi


def _shared_bufs(env):
    SH, sb = env["shared"], env["sb"]
    if "xnT" not in SH:
        SH["xnT"] = sb("b_xnT", [128, 8, L], BF16); SH["r_xnT"] = Reg()
        SH["kT"] = sb("b_kT", [128, 4, L], BF16); SH["r_kT"] = Reg()
        SH["vflat"] = sb("b_vflat", [128, NT * 512], BF16); SH["r_v"] = Reg()
        SH["mb"] = sb("b_mb", [128, 8, TB], BF16); SH["r_mb"] = Reg()
        SH["xrall"] = sb("b_xrall", [128, 4, D])
        SH["xr"] = [SH["xrall"][:, i, :] for i in range(4)]; SH["r_xr"] = [Reg() for _ in range(4)]
    return SH


def _unpack(env):
    class E:
        pass
    e = E()
    e.__dict__.update(env)
    return e


def build_stage_b(env):
    E = _unpack(env)
    P, nc, sb, banks, rbank, nb = E.P, E.nc, E.sb, E.banks, E.rbank, E.nb
    NSEQ, stages = E.NSEQ, E.stages
    wload, transpose_rows, load_norm_T = E.wload, E.transpose_rows, E.load_norm_T
    ones, r_const = E.ones, E.r_const
    X = mybir.AxisListType.X
    r_x1d = [Reg() for _ in range(NSEQ)]
    env["shared"]["r_x1d"] = r_x1d

    lv = sb("b_lv", [128, 4, 64])
    ltmp = sb("b_ltmp", [128, 2, 64])
    lst = sb("b_lst", [128, 8])
    subg = sb("b_subg", [128, 1])
    r_l = Reg()
    for i, a in enumerate((E.lq1, E.lk1, E.lq2, E.lk2)):
        P.dma("sp", lv[:, i:i + 1, :], a.partition_broadcast(128), pwrites=[r_l])
    P.dma("sp", subg[:], E.subln.rearrange("o e -> e o"), pwrites=[r_l])
    P.op("dve", lambda e: e.tensor_tensor(out=ltmp[:, 0, :], in0=lv[:, 0, :], in1=lv[:, 1, :], op=ALU.mult), reads=[r_l], writes=[r_l])
    P.op("dve", lambda e: e.tensor_tensor(out=ltmp[:, 1, :], in0=lv[:, 2, :], in1=lv[:, 3, :], op=ALU.mult), reads=[r_l], writes=[r_l])
    P.op("dve", lambda e: e.tensor_reduce(out=lst[:, 0:2], in_=ltmp[:], axis=X, op=ALU.add), reads=[r_l], writes=[r_l])
    P.op("act", lambda e: e.activation(out=lst[:, 2:4], in_=lst[:, 0:2], func=AF.Exp), reads=[r_l], writes=[r_l])
    P.op("dve", lambda e: e.tensor_tensor(out=lst[:, 4:5], in0=lst[:, 3:4], in1=lst[:, 2:3], op=ALU.subtract), reads=[r_l], writes=[r_l])
    P.op("dve", lambda e: e.tensor_scalar(out=lst[:, 5:6], in0=lst[:, 4:5], scalar1=-LAMBDA_INIT, scalar2=None, op0=ALU.add), reads=[r_l], writes=[r_l])
    P.op("dve", lambda e: e.tensor_scalar(out=subg[:], in0=subg[:], scalar1=1.0 - LAMBDA_INIT, scalar2=None, op0=ALU.mult), reads=[r_l], writes=[r_l])
    nlam = lst[:, 5:6]

    NTt = NT
    tpos = sb("b_tpos", [128, NTt])
    ang = sb("b_ang", [128, NTt, 8])
    angi = sb("b_angi", [128, NTt, 8], I32)
    angf = sb("b_angf", [128, NTt, 8])
    sh = sb("b_sh", [128, NTt, 8])
    sq = sb("b_sq", [128, NTt, 8])
    cosR = sb("b_cosR", [128, NTt, 8])
    sinR = sb("b_sinR", [128, NTt, 8])
    r_rope = Reg()
    P.op("pool", lambda e: e.iota(tpos[:], pattern=[[128, NTt]], base=0, channel_multiplier=1,
                                  allow_small_or_imprecise_dtypes=True), writes=[r_rope])
    for j in range(8):
        inv = 1.0 / (ROPE_THETA ** (2.0 * j / 16.0)) / (2.0 * math.pi)
        P.op("dve", lambda e, j=j, inv=inv: e.tensor_scalar(out=ang[:, :, j], in0=tpos[:], scalar1=inv, scalar2=None, op0=ALU.mult),
             reads=[r_rope], writes=[r_rope])
    P.op("dve", lambda e: e.tensor_copy(out=angi[:], in_=ang[:]), reads=[r_rope], writes=[r_rope])
    P.op("dve", lambda e: e.tensor_copy(out=angf[:], in_=angi[:]), reads=[r_rope], writes=[r_rope])
    P.op("dve", lambda e: e.tensor_tensor(out=ang[:], in0=ang[:], in1=angf[:], op=ALU.subtract), reads=[r_rope], writes=[r_rope])
    P.op("act", lambda e: e.activation(out=sh[:], in_=ang[:], func=AF.Sin, scale=math.pi), reads=[r_rope], writes=[r_rope])
    P.op("act", lambda e: e.activation(out=sq[:], in_=ang[:], func=AF.Sin, scale=math.pi / 2), reads=[r_rope], writes=[r_rope])
    P.op("dve", lambda e: e.tensor_tensor(out=sq[:], in0=sq[:], in1=sq[:], op=ALU.mult), reads=[r_rope], writes=[r_rope])
    P.op("dve", lambda e: e.tensor_scalar(out=sq[:], in0=sq[:], scalar1=-2.0, scalar2=1.0, op0=ALU.mult, op1=ALU.add), reads=[r_rope], writes=[r_rope])
    P.op("dve", lambda e: e.tensor_tensor(out=angf[:], in0=sh[:], in1=sq[:], op=ALU.mult), reads=[r_rope], writes=[r_rope])
    P.op("dve", lambda e: e.tensor_tensor(out=sh[:], in0=sh[:], in1=sh[:], op=ALU.mult), reads=[r_rope], writes=[r_rope])
    P.op("dve", lambda e: e.tensor_scalar(out=sinR[:], in0=angf[:], scalar1=2.0, scalar2=None, op0=ALU.mult),
         reads=[r_rope], writes=[r_rope])
    P.op("dve", lambda e: e.tensor_scalar(out=cosR[:], in0=sh[:], scalar1=-2.0, scalar2=1.0, op0=ALU.mult, op1=ALU.add),
         reads=[r_rope], writes=[r_rope])

    SH = _shared_bufs(env)
    xnT, r_xnT, kT, r_kT, r_v = SH["xnT"], SH["r_xnT"], SH["kT"], SH["r_kT"], SH["r_v"]
    vv = SH["vflat"][:].rearrange("p (i c) -> p i c", c=512)
    mnT = sb("b_mnT", [128, 8, NMEM], BF16); r_mnT = Reg()
    mkT = sb("b_mkT", [128, 4, NMEM], BF16); r_mkT = Reg()
    mv = sb("b_mv", [128, 2, 512], BF16); r_mv = Reg()
    qf = [sb("b_qf0", [128, 8, 64])[:]]; r_qf = [Reg(), Reg()]
    rt = [sb("b_rt0", [128, 4, 8, 8])[:]]
    qb = [sb("b_qb0", [128, 512], BF16)[:]]; r_qb = [Reg(), Reg()]
    qT = sb("b_qT", [128, 4, TB], BF16); r_qT = Reg()
    qT1 = mnT[:].rearrange("p a b -> p (a b)").rearrange("p (c t) -> p c t", t=TB)
    P.op("pool", lambda e: e.memset(qT[:], 0.0), writes=[r_qT])
    xqT = sb("b_xqT", [128, 4, TB], BF16); r_xqT = Reg()
    pT = [sb(f"b_pT{i}", [128, TB], BF16) for i in range(2)]; r_pT = [Reg() for _ in range(2)]
    at = [sb(f"b_at{i}", [128, TB]) for i in range(3)]; r_at = Reg()
    at.append(at[1])
    osq = sb("b_osq", [128, TB], BF16); r_osq = Reg()
    brA = sb("b_brA", [128, 4, TB], BF16); brB = sb("b_brB", [128, 4, TB], BF16)
    br = [brA, brA, brB]; _rA, _rB = Reg(), Reg(); r_br = [_rA, _rA, _rB]
    r_a = [r_at, Reg(), Reg()]
    sg = [at[0], at[1]]; r_sg = [r_a[0], r_a[1]]
    tm = [at[2]] * 2; r_tm = [r_a[2]] * 2
    mb, r_mb, xr, r_xr = SH["mb"], SH["r_mb"], SH["xr"], SH["r_xr"]
    env["shared"].update(qT=qT, r_qT=r_qT, pT=pT, r_pT=r_pT, at=at, r_at=r_at, brB=brB, brA=brA, xqT=xqT, mnT=mnT)
    env["shared"]["c"] = dict(x1t=xr, r_x1t=r_xr, xn2T=mb, r_xn2T=r_mb, sg=sg, r_sg=r_sg)
    cnt = {"q": 0, "p": 0, "s": 0}
    r_qT1z = Reg()
    r_atc = [r_at, r_at]

    def proj_rope_T(i, wt, rwt, dst_fn, r_dstT, extra_w=()):
        b = nb()
        for kc in range(8):
            P.op("pe", lambda e, kc=kc, b=b: e.matmul(banks[b][:, :], lhsT=xnT[:, kc, i * 128:(i + 1) * 128], rhs=wt[:, kc, :],
                                                      start=(kc == 0), stop=(kc == 7)), reads=[rwt, r_xnT], writes=[rbank[b]])
        s = cnt["q"] % 2
        cnt["q"] += 1
        q3 = qf[s]
        q3f = q3.rearrange("p c d -> p (c d)")
        P.op("act", lambda e, b=b, q3f=q3f: e.activation(out=q3f, in_=banks[b][:, :], func=AF.Copy),
             reads=[rbank[b]], writes=[r_qf[s]])
        c_ = cosR[:, i:i + 1, :].broadcast_to([128, 8, 8])
        s_ = sinR[:, i:i + 1, :].broadcast_to([128, 8, 8])
        x1, x2 = q3[:, :, 0:8], q3[:, :, 8:16]
        t = rt[s]
        for k, (a, tb_) in enumerate(((x1, c_), (x2, s_), (x2, c_), (x1, s_))):
            P.op("dve", lambda e, k=k, a=a, tb_=tb_, t=t: e.tensor_tensor(out=t[:, k, :, :], in0=a, in1=tb_, op=ALU.mult),
                 reads=[r_qf[s], r_rope], writes=[r_qf[s]])
        P.op("dve", lambda e, t=t, x1=x1: e.tensor_tensor(out=x1, in0=t[:, 0, :, :], in1=t[:, 1, :, :], op=ALU.subtract),
             reads=[r_qf[s]], writes=[r_qf[s]])
        P.op("dve", lambda e, t=t, x2=x2: e.tensor_tensor(out=x2, in0=t[:, 2, :, :], in1=t[:, 3, :, :], op=ALU.add),
             reads=[r_qf[s]], writes=[r_qf[s]])
        qbs = qb[s]
        P.op("dve", lambda e, q3f=q3f, qbs=qbs: e.tensor_copy(out=qbs, in_=q3f),
             reads=[r_qf[s]], writes=[r_qb[s]])
        return lambda: transpose_rows(qbs, r_qb[s], 512, dst_fn, r_dstT, evac="act", extra_w=extra_w)

    def attn(qTh, r_q, kfn, vfn, r_k, r_vv, nkt, scale, comp, first):
        pidx = {}

        def issue_s(kt):
            s = cnt["s"] % 2
            cnt["s"] += 1
            P.op("pe", lambda e, kt=kt, s=s: e.matmul(banks[s][:, :], lhsT=kfn(kt), rhs=qTh, start=True, stop=True),
                 reads=[r_k, r_q], writes=[rbank[s]])
            pTl = [pT[0][:], pT[1][:]] + pTx
            p = cnt["p"] % len(pTl)
            cnt["p"] += 1
            pidx[kt] = p
            pidx[kt] = (p, pTl[p])
            P.op("act", lambda e, s=s, pa=pTl[p]: e.activation(out=pa, in_=banks[s][:, :], func=AF.Exp, scale=scale),
                 reads=[rbank[s]], writes=[r_pTl[p]])

        def issue_pv(kt):
            p, pa = pidx[kt]
            P.op("pe", lambda e, kt=kt, pa=pa: e.matmul(banks[2 + comp][:, :], lhsT=vfn(kt), rhs=pa,
                                                      start=(kt == 0), stop=(kt == nkt - 1)),
                 reads=[r_vv, r_pTl[p]], writes=[rbank[2 + comp]])
            P.op("pe", lambda e, kt=kt, pa=pa: e.matmul(banks[4 + comp][:, :], lhsT=ones[:], rhs=pa,
                                                      start=(kt == 0), stop=(kt == nkt - 1)),
                 reads=[r_const, r_pTl[p]], writes=[rbank[4 + comp]])

        issue_s(0)
        for kt in range(nkt):
            if kt + 1 < nkt:
                issue_s(kt + 1)
            issue_pv(kt)

    bw = {}
    pTx = []
    r_pTl = list(r_pT) + [Reg()]

    def run_seq(seq, have_xnT):
        if "w" not in bw:
            extra = []
            if "aT" in env["shared"]:
                aTv = env["shared"]["aT"]
                extra = [aTv[:, 0:8, :], aTv[:, 8:16, :]]
                tail = aTv[:, 16:22, :].rearrange("p a b -> p (a b)")
                tf = tail[:, 0:2048].bitcast(F32)
                qf.append(tf[:, 0:512].rearrange("p (c d) -> p c d", d=64))
                rt.append(tf[:, 512:768].rearrange("p (k c d) -> p k c d", k=4, c=8))
                qb.append(tail[:, 2048:2560])
                pTx.append(tail[:, 2560:3072])
            else:
                qf.append(qf[0]); rt.append(rt[0]); qb.append(qb[0])
                r_qf[1] = r_qf[0]; r_qb[1] = r_qb[0]
            bw["w"] = env["mk_wpool"](env["base_slots"] + extra)
        wload = bw["w"]
        r_ys5d = env["shared"].get("r_ys5d", None)
        r0 = seq * L
        if not have_xnT:
            for i in range(NT):
                load_norm_T(r0 + i * 128, 0, xnT, r_xnT, i * 128)
        wk, rwk = wload("in", E.wb_in, 0, 8, 1024)
        wv, rwv = wload("in", E.wb_in, 0, 8, 1536)
        pend = None
        for i in range(NT):
            nxt = proj_rope_T(i, wk, rwk, lambda c0, n, i=i: [(slice(0, 128), kT[:, c0:c0 + n, i * 128:(i + 1) * 128])], r_kT)
            b = nb()
            for kc in range(8):
                P.op("pe", lambda e, kc=kc, b=b, i=i, wv=wv: e.matmul(banks[b][:, :], lhsT=xnT[:, kc, i * 128:(i + 1) * 128], rhs=wv[:, kc, :],
                                                               start=(kc == 0), stop=(kc == 7)), reads=[rwv, r_xnT], writes=[rbank[b]])
            P.op("dve", lambda e, b=b, i=i: e.tensor_copy(out=vv[:, i, :], in_=banks[b][:, :]), reads=[rbank[b]], pwrites=[r_v])
            if pend is not None:
                pend()
            pend = nxt
        pend()
        for mt in range(2):
            load_norm_T(seq * NMEM + mt * 128, 1, mnT, r_mnT, mt * 128, src_d=E.mem_d, extra_w=[r_qT])
        wmk, rwmk = wload("kv", E.wb_kv, 0, 8, 0)
        wmv, rwmv = wload("kv", E.wb_kv, 0, 8, 512)
        for h in range(4):
            b = nb()
            for kc in range(8):
                P.op("pe", lambda e, kc=kc, b=b, h=h, wmk=wmk: e.matmul(banks[b][:, 0:NMEM], lhsT=wmk[:, kc, h * 128:(h + 1) * 128], rhs=mnT[:, kc, :],
                                                               start=(kc == 0), stop=(kc == 7)), reads=[rwmk, r_mnT], writes=[rbank[b]])
            P.op("dve", lambda e, b=b, h=h: e.tensor_copy(out=mkT[:, h, :], in_=banks[b][:, 0:NMEM]), reads=[rbank[b]], pwrites=[r_mkT])
        for mt in range(2):
            b = nb()
            for kc in range(8):
                P.op("pe", lambda e, kc=kc, b=b, mt=mt, wmv=wmv: e.matmul(banks[b][:, :], lhsT=mnT[:, kc, mt * 128:(mt + 1) * 128], rhs=wmv[:, kc, :],
                                                                 start=(kc == 0), stop=(kc == 7)), reads=[rwmv, r_mnT], writes=[rbank[b]])
            P.op("dve", lambda e, b=b, mt=mt: e.tensor_copy(out=mv[:, mt, :], in_=banks[b][:, :]), reads=[rbank[b]], pwrites=[r_mv])

        P.op("pool", lambda e: e.memset(qT1[0:64, :, :], 0.0), writes=[r_mnT], pwrites=[r_qT])
        P.op("pool", lambda e: e.memset(qT[64:128, :, :], 0.0), pwrites=[r_qT])
        for blk in range(L // TB):
            t0 = blk * TB
            wq, rwq = wload("in", E.wb_in, 0, 8, 512)
            pend = None
            for tt in range(TB // 128):
                nxt = proj_rope_T(blk * (TB // 128) + tt, wq, rwq, lambda c0, n, tt=tt: [(slice(0, 64), qT[0:64, c0:c0 + n, tt * 128:(tt + 1) * 128]), (slice(64, 128), qT1[64:128, c0:c0 + n, tt * 128:(tt + 1) * 128])], r_qT, extra_w=[r_mnT])
                if pend is not None:
                    pend()
                pend = nxt
            wxq, rwxq = wload("in", E.wb_in, 0, 8, 2048)
            for c in range(4):
                b = nb()
                for kc in range(8):
                    P.op("pe", lambda e, kc=kc, b=b, c=c, wxq=wxq, t0=t0: e.matmul(banks[b][:, :], lhsT=wxq[:, kc, c * 128:(c + 1) * 128], rhs=xnT[:, kc, t0:t0 + TB],
                                                                   start=(kc == 0), stop=(kc == 7)), reads=[rwxq, r_xnT], writes=[rbank[b]])
                P.op("dve", lambda e, b=b, c=c: e.tensor_copy(out=xqT[:, c, :], in_=banks[b][:, :]), reads=[rbank[b]], pwrites=[r_xqT])
            pend()
            pending = [None]

            def flush():
                if pending[0] is not None:
                    pending[0]()
                    pending[0] = None

            def diff_tail(h):
                def run():
                    s = cnt["s"] % 2
                    cnt["s"] += 1
                    P.op("pe", lambda e, s=s: e.matmul(banks[s][:, :], lhsT=ones[:], rhs=osq[:], start=True, stop=True),
                         reads=[r_const, r_osq], writes=[rbank[s]])
                    P.op("act", lambda e, s=s: e.activation(out=at[2][:], in_=banks[s][:, :], func=AF.Ln, scale=1.0 / 128, bias=EPS),
                         reads=[rbank[s]], writes=[r_a[2]])
                    P.op("act", lambda e: e.activation(out=at[2][:], in_=at[2][:], func=AF.Exp, scale=-0.5), reads=[r_a[2]], writes=[r_a[2]])
                    P.op("dve", lambda e, h=h: e.scalar_tensor_tensor(out=br[1][:, h, :], in0=at[1][:], scalar=subg[:, 0:1], in1=at[2][:],
                                                                      op0=ALU.mult, op1=ALU.mult), reads=[r_a[1], r_a[2], r_l], pwrites=[r_br[1]])
                return run

            for h in range(4):
                for comp in range(2):
                    lo = 64 * comp
                    qsrc = (qT if comp == 0 else qT1)
                    attn((qsrc[:, h, :] if K128 else qsrc[lo:lo + 64, h, :]), r_qT,
                         (lambda kt, h=h: kT[:, h, kt * 128:(kt + 1) * 128]) if K128 else (lambda kt, h=h, lo=lo: kT[lo:lo + 64, h, kt * 128:(kt + 1) * 128]),
                         lambda kt, h=h: vv[:, kt, h * 128:(h + 1) * 128], r_kT, r_v, NT, 0.125, comp, True)
                    P.op("dve", lambda e, comp=comp: e.reciprocal(out=at[comp][:], in_=banks[4 + comp][:, :]), reads=[rbank[4 + comp]], writes=[r_a[comp]])
                    P.op("dve", lambda e, comp=comp: e.tensor_tensor(out=at[comp][:], in0=banks[2 + comp][:, :], in1=at[comp][:], op=ALU.mult),
                         reads=[rbank[2 + comp], r_a[comp]], writes=[r_a[comp]])
                    if comp == 0:
                        flush()
                P.op("dve", lambda e: e.scalar_tensor_tensor(out=at[1][:], in0=at[1][:], scalar=nlam, in1=at[0][:], op0=ALU.mult, op1=ALU.add),
                     reads=[r_a[0], r_a[1], r_l], writes=[r_a[1]])
                P.op("dve", lambda e: e.tensor_tensor(out=osq[:], in0=at[1][:], in1=at[1][:], op=ALU.mult), reads=[r_a[1]], writes=[r_osq])
                pending[0] = diff_tail(h)
            for h in range(4):
                attn(xqT[:, h, :], r_xqT,
                     lambda kt, h=h: mkT[:, h, kt * 128:(kt + 1) * 128],
                     lambda kt, h=h: mv[:, kt, h * 128:(h + 1) * 128], r_mkT, r_mv, 2, 128 ** -0.5, 0, True)
                P.op("dve", lambda e: e.reciprocal(out=at[0][:], in_=banks[4][:, :]), reads=[rbank[4]], writes=[r_a[0]])
                P.op("dve", lambda e, h=h: e.tensor_tensor(out=br[2][:, h, :], in0=banks[2][:, :], in1=at[0][:], op=ALU.mult),
                     reads=[rbank[2], r_a[0]], pwrites=[r_br[2]])
                flush()
            def merge(n, first, t0=t0):
                for half in range(2):
                    wu, rwu = wload("up", E.wb_up, n * 512, 4, half * 512)
                    wg, rwg = wload("in", E.wb_in, 0, 8, 2560 + n * 1024 + half * 512)
                    for m4 in range(4):
                        mc = half * 4 + m4
                        bu, bg = nb(), nb()
                        for kc in range(4):
                            P.op("pe", lambda e, kc=kc, bu=bu, m4=m4, n=n, wu=wu: e.matmul(banks[bu][:, :], lhsT=wu[:, kc, m4 * 128:(m4 + 1) * 128],
                                                                                           rhs=br[n][:, kc, :], start=(kc == 0), stop=(kc == 3)),
                                 reads=[rwu, r_br[n]], writes=[rbank[bu]])
                        for kc in range(8):
                            P.op("pe", lambda e, kc=kc, bg=bg, m4=m4, wg=wg, t0=t0: e.matmul(banks[bg][:, :], lhsT=wg[:, kc, m4 * 128:(m4 + 1) * 128],
                                                                                      rhs=xnT[:, kc, t0:t0 + TB], start=(kc == 0), stop=(kc == 7)),
                                 reads=[rwg, r_xnT], writes=[rbank[bg]])
                        s = cnt["q"] % 2
                        cnt["q"] += 1
                        P.op("act", lambda e, s=s, bg=bg: e.activation(out=sg[s][:], in_=banks[bg][:, :], func=AF.Sigmoid),
                             reads=[rbank[bg]], writes=[r_sg[s]])
                        if first:
                            P.op("dve", lambda e, s=s, bu=bu, mc=mc: e.tensor_tensor(out=mb[:, mc, :], in0=banks[bu][:, :], in1=sg[s][:], op=ALU.mult),
                                 reads=[rbank[bu], r_sg[s]], pwrites=[r_mb])
                        else:
                            P.op("dve", lambda e, s=s, bu=bu: e.tensor_tensor(out=tm[s][:], in0=banks[bu][:, :], in1=sg[s][:], op=ALU.mult),
                                 reads=[rbank[bu], r_sg[s]], writes=[r_tm[s]])
                            P.op("dve", lambda e, s=s, mc=mc: e.tensor_tensor(out=mb[:, mc, :], in0=mb[:, mc, :], in1=tm[s][:], op=ALU.add),
                                 reads=[r_tm[s], r_mb], pwrites=[r_mb])
            merge(1, True)
            if r_ys5d is not None:
                src = E.ys5_d[seq * 512:(seq + 1) * 512, t0:t0 + TB].rearrange("(c p) t -> p c t", p=128)
                P.dma("sp", br[0][:], src, reads=[r_ys5d[seq]], writes=[r_br[0]])
            else:
                P.op("pool", lambda e: e.memset(br[0][:], 0.0), writes=[r_br[0]])
            merge(2, False)
            merge(0, False)
            for tt in range(TB // 128):
                P.dma("sp", xr[tt][:], E.x_d[r0 + t0 + tt * 128: r0 + t0 + (tt + 1) * 128, :], writes=[r_xr[tt]])
            for half in range(2):
                wo, rwo = wload("out", E.wb_out, 0, 8, half * 512)
                for tt in range(TB // 128):
                    b = nb()
                    for kc in range(8):
                        P.op("pe", lambda e, kc=kc, b=b, tt=tt, wo=wo: e.matmul(banks[b][:, :], lhsT=mb[:, kc, tt * 128:(tt + 1) * 128], rhs=wo[:, kc, :],
                                                                                start=(kc == 0), stop=(kc == 7)), reads=[rwo, r_mb], writes=[rbank[b]])
                    P.op("dve", lambda e, b=b, tt=tt, half=half: e.tensor_tensor(out=xr[tt][:, half * 512:(half + 1) * 512], in0=banks[b][:, :],
                                                                                 in1=xr[tt][:, half * 512:(half + 1) * 512], op=ALU.add),
                         reads=[rbank[b], r_xr[tt]], writes=[r_xr[tt]])
            for tt in range(TB // 128):
                P.dma("pool", E.x1_d[r0 + t0 + tt * 128: r0 + t0 + (tt + 1) * 128, :], xr[tt][:], reads=[r_xr[tt]], pwrites=[r_x1d[seq]])

    env["shared"]["run_b_seq"] = run_seq


def build_stage_a(env):
    E = _unpack(env)
    P, nc, sb, banks, rbank, nb = E.P, E.nc, E.sb, E.banks, E.rbank, E.nb
    NSEQ = E.NSEQ
    wload, transpose_rows, load_norm_T = E.wload, E.transpose_rows, E.load_norm_T
    tbanks, rtb, ntb, ident, r_const = E.tbanks, E.rtb, E.ntb, E.ident, E.r_const
    SH = env["shared"]
    LOG = 11
    r_ys5d = [Reg() for _ in range(NSEQ)]
    SH["r_ys5d"] = r_ys5d

    _shared_bufs(env)
    xnT, r_xnT, kT, r_kT, vflat, r_v, mb, r_mb, xr, r_xr = (SH[k] for k in ("xnT", "r_xnT", "kT", "r_kT", "vflat", "r_v", "mb", "r_mb", "xr", "r_xr"))
    uT, r_uT = kT, r_kT
    zT, r_zT = vflat[:].rearrange("p (c t) -> p c t", t=L), r_v
    Xb, r_Xb = mb[:].rearrange("p a b -> p (a b)").rearrange("p (c t) -> p c t", t=L), r_mb
    if "aT" in SH:
        aTf = SH["aT"][:].rearrange("p a b -> p (a b)").bitcast(F32)
        r_X = SH["r_aT"]
    else:
        aTf = sb("a_Xf", [128, 5632])[:]
        r_X = Reg()
    Xslots = [aTf[:, 0:2 * L].rearrange("p (c t) -> p c t", t=L),
              SH["xrall"][:].rearrange("p a b -> p (a b)").rearrange("p (c t) -> p c t", t=L)]
    r_Xs = [r_X, Reg()]

    pp = E.xin[1][:, 0:768].rearrange("p (a b) -> p a b", b=32)
    ppi = sb("a_ppi", [128, 32], I32)
    r_pp = E.r_xin[1]
    Are = sb("a_Are", [128, LOG, 32]); Aim = sb("a_Aim", [128, LOG, 32]); Aimn = sb("a_Aimn", [128, LOG, 32])
    r_A = Reg()
    LAMR, LAMI, STP, AR, AI, MAG, YV, KF, FR, SHh, SQq, CH, SN, CS, AM1, NR, NI, DEN, BR, BI, NBI, T0, T1 = range(23)
    col = lambda i: pp[:, i, :]
    P.dma("sp", col(LAMR), E.lam_re[0].rearrange("d (p two) n -> (two n) (d p)", two=2), pwrites=[r_pp], allow_slow_non_contiguous=True)
    P.dma("sp", col(LAMI), E.lam_im[0].rearrange("d (p two) n -> (two n) (d p)", two=2), pwrites=[r_pp], allow_slow_non_contiguous=True)
    lsv = E.log_step[0].rearrange("d (p two) -> two (d p)", two=2)
    for g2 in range(2):
        P.dma("sp", pp[64 * g2:64 * g2 + 64, STP:STP + 1, :], lsv[g2:g2 + 1].partition_broadcast(64), pwrites=[r_pp], allow_slow_non_contiguous=True)

    def dv(fn, eng="dve"):
        P.op(eng, fn, reads=[r_pp], writes=[r_pp])

    TT, TS = "tensor_tensor", "tensor_scalar"
    dv(lambda e: e.activation(out=col(STP), in_=col(STP), func=AF.Exp), "act")
    dv(lambda e: e.tensor_tensor(out=col(AR), in0=col(LAMR), in1=col(STP), op=ALU.mult))
    dv(lambda e: e.tensor_tensor(out=col(AI), in0=col(LAMI), in1=col(STP), op=ALU.mult))
    dv(lambda e: e.activation(out=col(MAG), in_=col(AR), func=AF.Exp), "act")
    dv(lambda e: e.tensor_scalar(out=col(YV), in0=col(AI), scalar1=1.0 / (2 * math.pi), scalar2=None, op0=ALU.mult))
    dv(lambda e: e.tensor_copy(out=ppi[:], in_=col(YV)))
    dv(lambda e: e.tensor_copy(out=col(KF), in_=ppi[:]))
    dv(lambda e: e.tensor_tensor(out=col(FR), in0=col(YV), in1=col(KF), op=ALU.subtract))
    dv(lambda e: e.activation(out=col(SHh), in_=col(FR), func=AF.Sin, scale=math.pi), "act")
    dv(lambda e: e.activation(out=col(SQq), in_=col(FR), func=AF.Sin, scale=math.pi / 2), "act")
    dv(lambda e: e.tensor_tensor(out=col(CH), in0=col(SQq), in1=col(SQq), op=ALU.mult))
    dv(lambda e: e.tensor_scalar(out=col(CH), in0=col(CH), scalar1=-2.0, scalar2=1.0, op0=ALU.mult, op1=ALU.add))
    dv(lambda e: e.tensor_tensor(out=col(SN), in0=col(SHh), in1=col(CH), op=ALU.mult))
    dv(lambda e: e.tensor_scalar(out=col(SN), in0=col(SN), scalar1=2.0, scalar2=None, op0=ALU.mult))
    dv(lambda e: e.tensor_tensor(out=col(CS), in0=col(SHh), in1=col(SHh), op=ALU.mult))
    dv(lambda e: e.tensor_scalar(out=col(CS), in0=col(CS), scalar1=-2.0, scalar2=1.0, op0=ALU.mult, op1=ALU.add))
    P.op("dve", lambda e: e.tensor_tensor(out=Are[:, 0, :], in0=col(MAG), in1=col(CS), op=ALU.mult), reads=[r_pp], writes=[r_A])
    P.op("dve", lambda e: e.tensor_tensor(out=Aim[:, 0, :], in0=col(MAG), in1=col(SN), op=ALU.mult), reads=[r_pp], writes=[r_A])
    for s_ in range(1, LOG):
        P.op("dve", lambda e, s_=s_: e.tensor_tensor(out=col(T0), in0=Are[:, s_ - 1, :], in1=Are[:, s_ - 1, :], op=ALU.mult), reads=[r_A], writes=[r_pp])
        P.op("dve", lambda e, s_=s_: e.tensor_tensor(out=col(T1), in0=Aim[:, s_ - 1, :], in1=Aim[:, s_ - 1, :], op=ALU.mult), reads=[r_A, r_pp], writes=[r_pp])
        P.op("dve", lambda e, s_=s_: e.tensor_tensor(out=Are[:, s_, :], in0=col(T0), in1=col(T1), op=ALU.subtract), reads=[r_pp], writes=[r_A])
        P.op("dve", lambda e, s_=s_: e.tensor_tensor(out=col(T0), in0=Are[:, s_ - 1, :], in1=Aim[:, s_ - 1, :], op=ALU.mult), reads=[r_A, r_pp], writes=[r_pp])
        P.op("dve", lambda e, s_=s_: e.tensor_scalar(out=Aim[:, s_, :], in0=col(T0), scalar1=2.0, scalar2=None, op0=ALU.mult), reads=[r_pp], writes=[r_A])
    P.op("dve", lambda e: e.tensor_scalar(out=Aimn[:], in0=Aim[:], scalar1=-1.0, scalar2=None, op0=ALU.mult), reads=[r_A], writes=[r_A])
    P.op("dve", lambda e: e.tensor_scalar(out=col(AM1), in0=Are[:, 0, :], scalar1=-1.0, scalar2=None, op0=ALU.add), reads=[r_A, r_pp], writes=[r_pp])
    dv(lambda e: e.tensor_tensor(out=col(NR), in0=col(AM1), in1=col(LAMR), op=ALU.mult))
    P.op("dve", lambda e: e.tensor_tensor(out=col(T0), in0=Aim[:, 0, :], in1=col(LAMI), op=ALU.mult), reads=[r_A, r_pp], writes=[r_pp])
    dv(lambda e: e.tensor_tensor(out=col(NR), in0=col(NR), in1=col(T0), op=ALU.add))
    P.op("dve", lambda e: e.tensor_tensor(out=col(NI), in0=Aim[:, 0, :], in1=col(LAMR), op=ALU.mult), reads=[r_A, r_pp], writes=[r_pp])
    dv(lambda e: e.tensor_tensor(out=col(T0), in0=col(AM1), in1=col(LAMI), op=ALU.mult))
    dv(lambda e: e.tensor_tensor(out=col(NI), in0=col(NI), in1=col(T0), op=ALU.subtract))
    dv(lambda e: e.tensor_tensor(out=col(DEN), in0=col(LAMR), in1=col(LAMR), op=ALU.mult))
    dv(lambda e: e.tensor_tensor(out=col(T0), in0=col(LAMI), in1=col(LAMI), op=ALU.mult))
    dv(lambda e: e.tensor_tensor(out=col(DEN), in0=col(DEN), in1=col(T0), op=ALU.add))
    dv(lambda e: e.reciprocal(out=col(DEN), in_=col(DEN)))
    dv(lambda e: e.tensor_tensor(out=col(BR), in0=col(NR), in1=col(DEN), op=ALU.mult))
    dv(lambda e: e.tensor_tensor(out=col(BI), in0=col(NI), in1=col(DEN), op=ALU.mult))
    dv(lambda e: e.tensor_scalar(out=col(NBI), in0=col(BI), scalar1=-1.0, scalar2=None, op0=ALU.mult))

    Bre = xr[0][:].rearrange("p (c k) -> p c k", k=32)
    Bim = xr[1][:].rearrange("p (c k) -> p c k", k=32)
    xnb = E.xnb
    Bb = [xnb[0][:].rearrange("p (c k) -> p c k", k=32), xnb[1][:].rearrange("p (c k) -> p c k", k=32)]
    r_B = Reg()
    P.op("pool", lambda e: e.memset(xr[0][:], 0.0), writes=[r_xr[0]])
    P.op("pool", lambda e: e.memset(xr[1][:], 0.0), writes=[r_xr[1]])
    for g2 in range(2):
        P.dma("sp", Bre[64 * g2:64 * g2 + 64, :, 16 * g2:16 * g2 + 16],
              E.b_re[0].rearrange("d (p two) n h -> two n (d p) h", two=2)[g2], reads=[r_xr[0]], pwrites=[r_B])
        P.dma("sp", Bim[64 * g2:64 * g2 + 64, :, 16 * g2:16 * g2 + 16],
              E.b_im[0].rearrange("d (p two) n h -> two n (d p) h", two=2)[g2], reads=[r_xr[1]], pwrites=[r_B])
    r_Bb = Reg()
    for c in range(32):
        P.op("dve", lambda e, c=c: e.tensor_scalar(out=Bb[0][:, c, :], in0=Bre[:, c, :], scalar1=pp[:, BR, c:c + 1], scalar2=None, op0=ALU.mult),
             reads=[r_B, r_pp, E.r_xnb[0], r_xr[0], r_xr[1]], writes=[r_Bb, E.r_xnb[0]])
        P.op("dve", lambda e, c=c: e.scalar_tensor_tensor(out=Bb[0][:, c, :], in0=Bim[:, c, :], scalar=pp[:, NBI, c:c + 1], in1=Bb[0][:, c, :],
                                                          op0=ALU.mult, op1=ALU.add), reads=[r_B, r_pp, r_xr[0], r_xr[1]], writes=[r_Bb, E.r_xnb[0]])
        P.op("dve", lambda e, c=c: e.tensor_scalar(out=Bb[1][:, c, :], in0=Bre[:, c, :], scalar1=pp[:, BI, c:c + 1], scalar2=None, op0=ALU.mult),
             reads=[r_B, r_pp, E.r_xnb[1], r_xr[0], r_xr[1]], writes=[r_Bb, E.r_xnb[1]])
        P.op("dve", lambda e, c=c: e.scalar_tensor_tensor(out=Bb[1][:, c, :], in0=Bim[:, c, :], scalar=pp[:, BR, c:c + 1], in1=Bb[1][:, c, :],
                                                          op0=ALU.mult, op1=ALU.add), reads=[r_B, r_pp, r_xr[0], r_xr[1]], writes=[r_Bb, E.r_xnb[1]])
    BbT = sb("a_BbT", [128, 16, 128], BF16); r_BbT = Reg()
    CT = sb("a_CT", [128, 16, 128], BF16); r_CT = Reg()
    idx = lambda d_, part, cc: (d_ * 2 + part) * 4 + cc
    for d_ in range(2):
        for part in range(2):
            for cc in range(4):
                bi = ntb()
                c0 = d_ * 16 + 4 * cc
                src = xnb[part][:, c0 * 32:(c0 + 4) * 32]
                P.op("pe", lambda e, bi=bi, src=src: e.transpose(tbanks[bi][:, 0:128], src, ident[:]),
                     reads=[r_Bb, r_const, E.r_xnb[part]], writes=[rtb[bi]])
                P.op("dve", lambda e, bi=bi, k=idx(d_, part, cc): e.tensor_copy(out=BbT[:, k, :], in_=tbanks[bi][:, 0:128]),
                     reads=[rtb[bi]], pwrites=[r_BbT])

    P.op("dve", lambda e: e.tensor_tensor(out=col(AR), in0=Are[:, 1, :], in1=Are[:, 0, :], op=ALU.mult), reads=[r_A, r_pp], writes=[r_pp])
    P.op("dve", lambda e: e.tensor_tensor(out=col(T0), in0=Aim[:, 1, :], in1=Aim[:, 0, :], op=ALU.mult), reads=[r_A, r_pp], writes=[r_pp])
    P.op("dve", lambda e: e.tensor_tensor(out=col(AR), in0=col(AR), in1=col(T0), op=ALU.subtract), reads=[r_pp], writes=[r_pp])
    P.op("dve", lambda e: e.tensor_tensor(out=col(AI), in0=Are[:, 1, :], in1=Aim[:, 0, :], op=ALU.mult), reads=[r_A, r_pp], writes=[r_pp])
    P.op("dve", lambda e: e.tensor_tensor(out=col(T0), in0=Aim[:, 1, :], in1=Are[:, 0, :], op=ALU.mult), reads=[r_A, r_pp], writes=[r_pp])
    P.op("dve", lambda e: e.tensor_tensor(out=col(AI), in0=col(AI), in1=col(T0), op=ALU.add), reads=[r_pp], writes=[r_pp])
    P.op("dve", lambda e: e.tensor_scalar(out=col(T1), in0=col(AI), scalar1=-1.0, scalar2=None, op0=ALU.mult), reads=[r_pp], writes=[r_pp])
    FOLD0 = "at" in SH
    FOLD1 = FOLD0 and ("qT" in SH) and ("mnT" in SH)
    BbT2 = None
    BbTP = []
    if FOLD0:
        BbT2 = sb("a_BbT2", [128, 16, 128], BF16); r_BbT2 = Reg()
        BbTP = [BbT2[:]]
        bbts = []
        if FOLD1:
            bbts = [SH["qT"], SH["mnT"]]
            BbTP += [t_[:].rearrange("p a b -> p (a b)").rearrange("p (k c) -> p k c", c=128) for t_ in bbts]
        at_ = SH["at"]
        B2 = [at_[0][:].bitcast(BF16), at_[1][:].bitcast(BF16)]
        B2v = [B2[0].rearrange("p (c k) -> p c k", k=32), B2[1].rearrange("p (c k) -> p c k", k=32)]
        r_B2 = SH["r_at"]
        pwb = [(Are[:, 0, :], Aim[:, 0, :], Aimn[:, 0, :]), (Are[:, 1, :], Aim[:, 1, :], Aimn[:, 1, :]), (col(AR), col(AI), col(T1))]
        for m in range(len(BbTP)):
            pa, pb, pnb = pwb[m]
            for c in range(32):
                P.op("dve", lambda e, c=c, pa=pa: e.tensor_scalar(out=B2v[0][:, c, :], in0=Bb[0][:, c, :], scalar1=pa[:, c:c + 1], scalar2=None, op0=ALU.mult),
                     reads=[r_Bb, r_A, r_pp, E.r_xnb[0], E.r_xnb[1]], writes=[r_B2])
                P.op("dve", lambda e, c=c, pnb=pnb: e.scalar_tensor_tensor(out=B2v[0][:, c, :], in0=Bb[1][:, c, :], scalar=pnb[:, c:c + 1], in1=B2v[0][:, c, :],
                                                                  op0=ALU.mult, op1=ALU.add), reads=[r_Bb, r_A, r_pp, E.r_xnb[0], E.r_xnb[1]], writes=[r_B2])
                P.op("dve", lambda e, c=c, pa=pa: e.tensor_scalar(out=B2v[1][:, c, :], in0=Bb[1][:, c, :], scalar1=pa[:, c:c + 1], scalar2=None, op0=ALU.mult),
                     reads=[r_Bb, r_A, r_pp, E.r_xnb[0], E.r_xnb[1]], writes=[r_B2])
                P.op("dve", lambda e, c=c, pb=pb: e.scalar_tensor_tensor(out=B2v[1][:, c, :], in0=Bb[0][:, c, :], scalar=pb[:, c:c + 1], in1=B2v[1][:, c, :],
                                                                  op0=ALU.mult, op1=ALU.add), reads=[r_Bb, r_A, r_pp, E.r_xnb[0], E.r_xnb[1]], writes=[r_B2])
            for d_ in range(2):
                for part in range(2):
                    for cc in range(4):
                        bi = ntb()
                        c0 = d_ * 16 + 4 * cc
                        src = B2[part][:, c0 * 32:(c0 + 4) * 32]
                        P.op("pe", lambda e, bi=bi, src=src: e.transpose(tbanks[bi][:, 0:128], src, ident[:]),
                             reads=[r_B2, r_const], writes=[rtb[bi]])
                        P.op("dve", lambda e, bi=bi, k=idx(d_, part, cc), m=m: e.tensor_copy(out=BbTP[m][:, k, :], in_=tbanks[bi][:, 0:128]),
                             reads=[rtb[bi]] + ([SH["r_qT"]] if (FOLD1 and m >= 1) else []), pwrites=[r_BbT2])
        r_bbd = Reg()
        for m in range(1, len(BbTP)):
            P.dma("sp", E.bb_d[m - 1], bbts[m - 1][:].rearrange("p a b -> p (a b)"), reads=[r_BbT2, SH["r_qT"]], pwrites=[r_bbd])

    Cn = [xr[2][:, 0:512].rearrange("p (c n) -> p c n", n=64), xr[3][:, 0:512].rearrange("p (c n) -> p c n", n=64)]
    r_Cn = Reg()
    for part, csrc in enumerate((E.c_re, E.c_im)):
        P.dma("sp", Cn[part], csrc[0].rearrange("d (cc g8) h n -> (g8 h) (d cc) n", g8=8), reads=[r_xr[2 + part]], writes=[r_xr[2 + part]])
    msk = sb("a_msk", [128, 4])
    r_m = Reg()
    P.dma("sp", msk[:], E.cmask, writes=[r_m])
    if "qT" in SH:
        in2 = SH["qT"][:].rearrange("p a b -> p (a b)").rearrange("p (a b c) -> p a b c", a=2, b=8)
        r_in2 = SH["r_qT"]
    else:
        in2 = sb("a_in2", [128, 2, 8, 128], BF16)[:]
        r_in2 = Reg()
    for part in range(2):
        for g2 in range(2):
            mcol = msk[:, 2 * part + g2: 2 * part + g2 + 1]
            P.op("dve", lambda e, part=part, g2=g2, mcol=mcol: e.tensor_scalar(out=in2[:, part, :, 64 * g2:64 * g2 + 64], in0=Cn[part], scalar1=mcol,
                                                                               scalar2=None, op0=ALU.mult),
                 reads=[r_xr[2 + part], r_m], pwrites=[r_in2])
    for d_ in range(2):
        for part in range(2):
            for cc in range(4):
                bi = ntb()
                P.op("pe", lambda e, bi=bi, part=part, k=d_ * 4 + cc: e.transpose(tbanks[bi][:, 0:128], in2[:, part, k, :], ident[:]),
                     reads=[r_in2, r_const], writes=[rtb[bi]])
                P.op("dve", lambda e, bi=bi, k=idx(d_, part, cc): e.tensor_copy(out=CT[:, k, :], in_=tbanks[bi][:, 0:128]),
                     reads=[rtb[bi]], pwrites=[r_CT])
    FOLDD = all(k in SH for k in ("brB", "brA", "xqT"))
    npow = 3 if FOLDD else 1
    if FOLDD:
        ctts = [SH["brB"], SH["brA"], SH["xqT"]]
    elif "brB" in SH:
        ctts = [SH["brB"]]
    else:
        ctts = [sb("a_ct2", [128, 4, TB], BF16)]
    CTP = [t_[:].rearrange("p a b -> p (a b)").rearrange("p (k c) -> p k c", c=128) for t_ in ctts]
    r_CT2 = Reg()
    pw = [(Are[:, 0, :], Aim[:, 0, :], Aimn[:, 0, :]), (Are[:, 1, :], Aim[:, 1, :], Aimn[:, 1, :]), (col(AR), col(AI), col(T1))]
    for m in range(npow):
        pa, pb, pnb = pw[m]
        C2 = CTP[m]
        for d_ in range(2):
            for cc in range(4):
                k0, k1 = idx(d_, 0, cc), idx(d_, 1, cc)
                for q in range(4):
                    ci = d_ * 16 + 4 * cc + q
                    sl = slice(32 * q, 32 * q + 32)
                    P.op("dve", lambda e, k0=k0, sl=sl, ci=ci, C2=C2, pa=pa: e.tensor_scalar(out=C2[:, k0, sl], in0=CT[:, k0, sl], scalar1=pa[:, ci:ci + 1], scalar2=None, op0=ALU.mult),
                         reads=[r_CT, r_A, r_pp], pwrites=[r_CT2])
                    P.op("dve", lambda e, k0=k0, k1=k1, sl=sl, ci=ci, C2=C2, pb=pb: e.scalar_tensor_tensor(out=C2[:, k0, sl], in0=CT[:, k1, sl], scalar=pb[:, ci:ci + 1], in1=C2[:, k0, sl],
                                                                                            op0=ALU.mult, op1=ALU.add), reads=[r_CT, r_A, r_pp, r_CT2], pwrites=[r_CT2])
                    P.op("dve", lambda e, k1=k1, sl=sl, ci=ci, C2=C2, pa=pa: e.tensor_scalar(out=C2[:, k1, sl], in0=CT[:, k1, sl], scalar1=pa[:, ci:ci + 1], scalar2=None, op0=ALU.mult),
                         reads=[r_CT, r_A, r_pp], pwrites=[r_CT2])
                    P.op("dve", lambda e, k0=k0, k1=k1, sl=sl, ci=ci, C2=C2, pnb=pnb: e.scalar_tensor_tensor(out=C2[:, k1, sl], in0=CT[:, k0, sl], scalar=pnb[:, ci:ci + 1], in1=C2[:, k1, sl],
                                                                                            op0=ALU.mult, op1=ALU.add), reads=[r_CT, r_A, r_pp, r_CT2], pwrites=[r_CT2])
    r_ct2d = Reg()
    for m in range(npow):
        P.dma("sp", E.ct2_d[m], ctts[m][:].rearrange("p a b -> p (a b)"), reads=[r_CT2], pwrites=[r_ct2d])
    CT2 = CTP[0]
    dcol = sb("a_dcol", [128, 8]); r_dc = Reg()
    P.dma("sp", dcol[:, 0:4], E.s5_d[0].rearrange("(c p) -> p c", p=128), pwrites=[r_dc], allow_slow_non_contiguous=True)
    P.dma("sp", dcol[:, 4:8], E.b_glu[0].rearrange("(c p) -> p c", p=128), pwrites=[r_dc], allow_slow_non_contiguous=True)

    gl = [aTf[:, 2 * L + i * TB: 2 * L + (i + 1) * TB] for i in range(3)]; r_gl = [Reg(), Reg(), Reg()]
    if E.debug:
        def dbg(name, ap, shape, dt, reads):
            dd = nc.dram_tensor(name, list(shape), dt, kind="ExternalOutput").ap()
            P.dma("sp", dd, ap, reads=reads, final=True)
        dbg("dbg_Are", Are[:].rearrange("p a b -> p (a b)"), [128, LOG * 32], F32, [r_A])
        dbg("dbg_Aim", Aim[:].rearrange("p a b -> p (a b)"), [128, LOG * 32], F32, [r_A])
        dbg("dbg_pp", E.xin[1][:, 0:768], [128, 768], F32, [r_pp])
        dbg("dbg_BbT", BbT[:].rearrange("p a b -> p (a b)"), [128, 16 * 128], BF16, [r_BbT])
        dbg("dbg_CT", CT[:].rearrange("p a b -> p (a b)"), [128, 16 * 128], BF16, [r_CT])
    if "pT" in SH:
        yst = [SH["pT"][0][:], SH["pT"][1][:]]; r_yst = SH["r_pT"]
    else:
        yst = [sb(f"a_yst{i}", [128, TB], BF16)[:] for i in range(2)]; r_yst = [Reg(), Reg()]
    GC = 2.0 * math.sqrt(2.0 / math.pi)
    cn = {"e": 0, "y": 0, "x": 0}

    def scan(ci, rev, eng, Xs, r_X):
        def sl(part, start, n, S):
            return Xs[:, part, start:start + (n - 1) * S + 1:S]

        st = {"t3": None, "t4": None, "first": True}

        def level(d, dst0, src0, n, S):
            a = Are[:, d, ci:ci + 1]
            b = Aim[:, d, ci:ci + 1]
            nbm = Aimn[:, d, ci:ci + 1]
            rd, idd = sl(0, dst0, n, S), sl(1, dst0, n, S)
            rs, is_ = sl(0, src0, n, S), sl(1, src0, n, S)

            def stt(o, i0, sc, deps, first):
                return P.op(eng, lambda e, o=o, i0=i0, sc=sc: e.scalar_tensor_tensor(out=o, in0=i0, scalar=sc, in1=o, op0=ALU.mult, op1=ALU.add),
                            reads=([r_A, r_X] if first else [r_A]), deps=deps)

            f = st["first"]
            t1 = stt(rd, rs, a, [st["t3"]] if st["t3"] else [], f)
            t2 = stt(idd, is_, a, [st["t4"]] if st["t4"] else [], f)
            st["t3"] = stt(rd, is_, nbm, [t1], False)
            st["t4"] = stt(idd, rs, b, [t2], False)
            st["first"] = False

        for d in range(2 if FOLD1 else (1 if FOLD0 else 0), LOG):
            S = 2 << d
            h = 1 << d
            n = L // S
            if not rev:
                level(d, S - 1, h - 1, n, S)
            else:
                level(d, 0, h, n, S)
        for d in range(LOG - 2, (1 if FOLDD else 0), -1):
            S = 2 << d
            h = 1 << d
            n = L // S - 1
            if not rev:
                level(d, S + h - 1, S - 1, n, S)
            else:
                level(d, S - h, S, n, S)
        k, v = st["t4"]
        r_X.w = {k: v}
        r_X.rs = {}

    for seq in range(NSEQ):
        P.barrier()
        for m in range(npow):
            P.dma("sp", ctts[m][:].rearrange("p a b -> p (a b)"), E.ct2_d[m], reads=[r_ct2d], pwrites=[r_CT2])
        if FOLD1:
            for m in range(2):
                P.dma("sp", bbts[m][:].rearrange("p a b -> p (a b)"), E.bb_d[m], reads=[r_bbd], pwrites=[r_BbT2])
        r0 = seq * L
        for i in range(NT):
            load_norm_T(r0 + i * 128, 0, xnT, r_xnT, i * 128)
        wu, rwu = wload("in", E.wb_in, 0, 8, 0)
        for c in range(4):
            for tb in range(4):
                b = nb()
                for kc in range(8):
                    P.op("pe", lambda e, kc=kc, b=b, c=c, tb=tb, wu=wu: e.matmul(banks[b][:, :], lhsT=wu[:, kc, c * 128:(c + 1) * 128],
                                                                                 rhs=xnT[:, kc, tb * 512:(tb + 1) * 512], start=(kc == 0), stop=(kc == 7)),
                         reads=[rwu, r_xnT], writes=[rbank[b]])
                P.op("act", lambda e, b=b, c=c, tb=tb: e.activation(out=uT[:, c, tb * 512:(tb + 1) * 512], in_=banks[b][:, :], func=AF.Copy),
                     reads=[rbank[b]], pwrites=[r_uT])
        tiles = [(cc, q, d_) for cc in range(4) for q in range(4) for d_ in range(2)]
        slots = {}

        def front(ti):
            cc, q, d_ = tiles[ti]
            xsl = cn["x"] % 2
            cn["x"] += 1
            slots[ti] = xsl
            Xs, r_X = Xslots[xsl], r_Xs[xsl]
            for part in range(2):
                if FOLD1:
                    if d_ == 0:
                        cls = {0: [(0, 0)], 1: [(0, 1), (1, 0)], 2: [(0, 2)], 3: [(0, 3), (1, 2), (2, 1), (3, 0)]}
                    else:
                        cls = {3: [(0, 3)], 2: [(0, 2), (1, 3)], 1: [(0, 1)], 0: [(0, 0), (1, 1), (2, 2), (3, 3)]}
                    for c in range(4):
                        s = cn["e"] % 2
                        cn["e"] += 1
                        terms = cls[c]
                        for ti_, (m, u0) in enumerate(terms):
                            wsel = BbT if m == 0 else BbTP[m - 1]
                            usl = slice(u0, u0 + 2045, 4)
                            P.op("pe", lambda e, s=s, q=q, k=idx(d_, part, cc), cc=cc, usl=usl, wsel=wsel, first=(ti_ == 0), last=(ti_ == len(terms) - 1): e.matmul(
                                banks[s][:, :], lhsT=wsel[32 * q:32 * q + 32, k, :], rhs=uT[32 * q:32 * q + 32, cc, usl],
                                start=first, stop=last, tile_position=(32 * q, 0)), reads=[r_BbT, r_BbT2, r_uT], writes=[rbank[s]])
                        osl = slice(c, c + 2045, 4)
                        P.op("act", lambda e, s=s, part=part, osl=osl, Xs=Xs: e.activation(out=Xs[:, part, osl], in_=banks[s][:, :], func=AF.Copy),
                             reads=[rbank[s]], pwrites=[r_X])
                    continue
                if not FOLD0:
                    groups = [(tb * 512, 1, tb * 512, None) for tb in range(4)]
                else:
                    pd = 1 if d_ == 0 else 0
                    groups = []
                    for jb in range(2):
                        groups.append((jb * 1024 + (1 - pd), 2, jb * 1024 + (1 - pd), None))
                        groups.append((jb * 1024 + pd, 2, jb * 1024 + pd, jb * 1024 + (1 - pd)))
                for (o0, st_, u0, u1) in groups:
                    s = cn["e"] % 2
                    cn["e"] += 1
                    usl = slice(u0, u0 + 511 * st_ + 1, st_)
                    osl = slice(o0, o0 + 511 * st_ + 1, st_)
                    P.op("pe", lambda e, s=s, q=q, k=idx(d_, part, cc), cc=cc, usl=usl, last=(u1 is None): e.matmul(
                        banks[s][:, :], lhsT=BbT[32 * q:32 * q + 32, k, :], rhs=uT[32 * q:32 * q + 32, cc, usl],
                        start=True, stop=last, tile_position=(32 * q, 0)), reads=[r_BbT, r_uT], writes=[rbank[s]])
                    if u1 is not None:
                        usl2 = slice(u1, u1 + 511 * st_ + 1, st_)
                        P.op("pe", lambda e, s=s, q=q, k=idx(d_, part, cc), cc=cc, usl2=usl2: e.matmul(
                            banks[s][:, :], lhsT=BbT2[32 * q:32 * q + 32, k, :], rhs=uT[32 * q:32 * q + 32, cc, usl2],
                            start=False, stop=True, tile_position=(32 * q, 0)), reads=[r_BbT2, r_uT], writes=[rbank[s]])
                    P.op("act", lambda e, s=s, part=part, osl=osl, Xs=Xs: e.activation(out=Xs[:, part, osl], in_=banks[s][:, :], func=AF.Copy),
                         reads=[rbank[s]], pwrites=[r_X])

        def back(ti):
            cc, q, d_ = tiles[ti]
            xsl = slots[ti]
            Xs, r_X = Xslots[xsl], r_Xs[xsl]
            ci = d_ * 16 + 4 * cc + q
            scan(ci, d_ == 1, "dve", Xs, r_X)
            P.op("act", lambda e, Xs=Xs: e.activation(out=Xb, in_=Xs, func=AF.Copy), reads=[r_X], writes=[r_Xb])
            if FOLDD:
                if d_ == 0:
                    extra = {3: [], 1: [(1, -2, 1, 511)], 0: [(0, -1, 1, 511)], 2: [(0, -1, 0, 512), (2, -3, 1, 511)]}
                else:
                    extra = {0: [], 2: [(1, 2, 0, 511)], 3: [(0, 1, 0, 511)], 1: [(0, 1, 0, 512), (2, 3, 0, 511)]}
                for c in range(4):
                    bk = 2 + c
                    for part in range(2):
                        first = (d_ == 0 and part == 0)
                        last = (d_ == 1 and part == 1)
                        ex = extra[c]
                        P.op("pe", lambda e, q=q, k=idx(d_, part, cc), bk=bk, c=c, part=part, first=first, last=(last and not ex): e.matmul(
                            banks[bk][32 * q:32 * q + 32, :], lhsT=CT[:, k, 32 * q:32 * q + 32], rhs=Xb[:, part, c:c + 2045:4],
                            start=first, stop=last, tile_position=(0, 32 * q)), reads=[r_CT, r_Xb], writes=[rbank[bk]])
                        for ei, (m, sh_, j0, ncol) in enumerate(ex):
                            src0 = 4 * j0 + c + sh_
                            P.op("pe", lambda e, q=q, k=idx(d_, part, cc), bk=bk, part=part, m=m, j0=j0, src0=src0, ncol=ncol, last=(last and ei == len(ex) - 1): e.matmul(
                                banks[bk][32 * q:32 * q + 32, j0:j0 + ncol], lhsT=CTP[m][:, k, 32 * q:32 * q + 32],
                                rhs=Xb[:, part, src0:src0 + 4 * (ncol - 1) + 1:4],
                                start=False, stop=last, tile_position=(0, 32 * q)), reads=[r_CT2, r_Xb], writes=[rbank[bk]])
                return
            fpar = 0 if d_ == 0 else 1
            for jb in range(2):
                for par in range(2):
                    bk = 2 + jb * 2 + par
                    base = jb * 1024 + par
                    for part in range(2):
                        first = (d_ == 0 and part == 0)
                        last = (d_ == 1 and part == 1)
                        folded = (par == fpar)
                        P.op("pe", lambda e, q=q, k=idx(d_, part, cc), bk=bk, base=base, part=part, first=first, last=(last and not folded): e.matmul(
                            banks[bk][32 * q:32 * q + 32, :], lhsT=CT[:, k, 32 * q:32 * q + 32], rhs=Xb[:, part, base:base + 1023:2],
                            start=first, stop=last, tile_position=(0, 32 * q)), reads=[r_CT, r_Xb], writes=[rbank[bk]])
                        if folded:
                            if d_ == 0:
                                c0 = 1 if jb == 0 else 0
                                src0 = base + 2 * c0 - 1
                                ncol = 512 - c0
                            else:
                                c0 = 0
                                src0 = base + 1
                                ncol = 512 if jb == 0 else 511
                            P.op("pe", lambda e, q=q, k=idx(d_, part, cc), bk=bk, part=part, c0=c0, src0=src0, ncol=ncol, last=last: e.matmul(
                                banks[bk][32 * q:32 * q + 32, c0:c0 + ncol], lhsT=CT2[:, k, 32 * q:32 * q + 32],
                                rhs=Xb[:, part, src0:src0 + 2 * (ncol - 1) + 1:2],
                                start=False, stop=last, tile_position=(0, 32 * q)), reads=[r_CT2, r_Xb], writes=[rbank[bk]])

        def gelu_chunk(cc):
            for tb in range(4):
                jb_, par_ = tb // 2, tb % 2
                tsl = slice(tb, tb + 2045, 4) if FOLDD else slice(jb_ * 1024 + par_, jb_ * 1024 + par_ + 1023, 2)
                P.op("dve", lambda e, tb=tb, cc=cc, tsl=tsl: e.scalar_tensor_tensor(out=gl[0], in0=uT[:, cc, tsl], scalar=dcol[:, cc:cc + 1], in1=banks[2 + tb][:, :],
                                                                                   op0=ALU.mult, op1=ALU.add), reads=[r_uT, r_dc, rbank[2 + tb]], writes=[r_gl[0]])
                P.op("act", lambda e: e.activation(out=gl[1], in_=gl[0], func=AF.Square), reads=[r_gl[0]], writes=[r_gl[1]])
                P.op("dve", lambda e: e.tensor_scalar(out=gl[1], in0=gl[1], scalar1=0.044715, scalar2=1.0, op0=ALU.mult, op1=ALU.add), reads=[r_gl[1]], writes=[r_gl[1]])
                P.op("dve", lambda e: e.tensor_tensor(out=gl[1], in0=gl[1], in1=gl[0], op=ALU.mult), reads=[r_gl[0], r_gl[1]], writes=[r_gl[1]])
                P.op("act", lambda e: e.activation(out=gl[2], in_=gl[1], func=AF.Sigmoid, scale=GC), reads=[r_gl[1]], writes=[r_gl[2]])
                P.op("dve", lambda e, cc=cc, tsl=tsl: e.tensor_tensor(out=zT[:, cc, tsl], in0=gl[0], in1=gl[2], op=ALU.mult), reads=[r_gl[0], r_gl[2]], pwrites=[r_zT])

        front(0)
        for ti in range(len(tiles)):
            if ti + 1 < len(tiles):
                front(ti + 1)
            back(ti)
            if ti % 8 == 7:
                gelu_chunk(tiles[ti][0])
        wgl, rwgl = wload("glu", E.wb_glu, 0, 4, 0)
        for mc in range(4):
            for tb in range(4):
                b = nb()
                tsl = slice(tb * 512, (tb + 1) * 512)
                for kc in range(4):
                    P.op("pe", lambda e, kc=kc, b=b, mc=mc, tsl=tsl, wgl=wgl: e.matmul(banks[b][:, :], lhsT=wgl[:, kc, mc * 128:(mc + 1) * 128], rhs=zT[:, kc, tsl],
                                                                                       start=(kc == 0), stop=(kc == 3)), reads=[rwgl, r_zT], writes=[rbank[b]])
                P.op("act", lambda e, b=b, mc=mc: e.activation(out=gl[2], in_=banks[b][:, :], func=AF.Sigmoid, bias=dcol[:, 4 + mc:5 + mc]),
                     reads=[rbank[b], r_dc], writes=[r_gl[2]])
                y = cn["y"] % 2
                cn["y"] += 1
                P.op("dve", lambda e, y=y, mc=mc, tsl=tsl: e.tensor_tensor(out=yst[y], in0=zT[:, mc, tsl], in1=gl[2], op=ALU.mult),
                     reads=[r_zT, r_gl[2]], writes=[r_yst[y]])
                P.dma("pool", E.ys5_d[seq * 512 + mc * 128: seq * 512 + (mc + 1) * 128, tb * 512:(tb + 1) * 512], yst[y],
                      reads=[r_yst[y]], pwrites=[r_ys5d[seq]])
        P.barrier()
        if "B" in E.stages:
            SH["run_b_seq"](seq, True)
        if "C" in E.stages:
            SH["run_c_seq"](seq)


_PARAM_KEYS = ["norm_mix", "norm_mem", "norm_ffn", "w_in", "s5_lambda_re", "s5_lambda_im", "s5_log_step",
               "s5_b_re", "s5_b_im", "s5_c_re", "s5_c_im", "s5_d", "s5_w_glu", "s5_b_glu",
               "diff_lambda_q1", "diff_lambda_k1", "diff_lambda_q2", "diff_lambda_k2", "diff_subln",
               "w_mem_kv", "w_up", "w_out", "w_ffn_in", "w_ffn_out"]


def _core_inputs(inputs, core, nseq):
    f = lambda a: np.ascontiguousarray(np.asarray(a, dtype=np.float32))
    x_all = inputs["_x_all"]
    m_all = inputs["_m_all"]
    m = {"x": x_all[core * nseq:(core + 1) * nseq].reshape(nseq * L, D),
         "mem": m_all[core * nseq:(core + 1) * nseq].reshape(nseq * NMEM, D)}
    for k in _PARAM_KEYS:
        a = f(inputs[k])
        if k in ("w_in", "w_mem_kv", "w_out", "w_ffn_in", "w_ffn_out", "s5_w_glu"):
            a = a[0]
        elif k == "w_up":
            a = a[0].reshape(3 * 512, D)
        m[k] = np.ascontiguousarray(a)
    m["norm_final"] = f(inputs["norm_final"]).reshape(1, D)
    par = ((np.arange(128) // 16) % 2).astype(np.float32)
    m["cmask"] = np.stack([1 - par, par, par - 1, -par], axis=1).astype(np.float32)
    return m


def kernel(**inputs):
    xp = np.asarray(inputs["x_prompt"], dtype=np.float32)
    xs = np.asarray(inputs["x_sample"], dtype=np.float32)
    mp = np.asarray(inputs["mem_prompt"], dtype=np.float32)
    ms = np.asarray(inputs["mem_sample"], dtype=np.float32)
    inputs = dict(inputs)
    inputs["_x_all"] = np.concatenate([xp, xs], axis=0)
    inputs["_m_all"] = np.concatenate([mp, ms], axis=0)
    nseq = inputs["_x_all"].shape[0] // N_CORES
    nc = build_nc(nseq)
    in_maps = [_core_inputs(inputs, c, nseq) for c in range(N_CORES)]
    res = run_bass_kernel_spmd(nc, in_maps, core_ids=list(range(N_CORES)))
    y = np.concatenate([r["y"].reshape(nseq, L, D) for r in res.results], axis=0).astype(np.float32)
    return (np.ascontiguousarray(y[:xp.shape[0]]), np.ascontiguousarray(y[xp.shape[0]:]))
```

```python
import math
from contextlib import ExitStack

import numpy as np
import concourse.bass as bass
import concourse.mybir as mybir
from concourse.bass_utils import run_bass_kernel_spmd
from concourse.alu_op_type import AluOpType as ALU

F32 = mybir.dt.float32
BF16 = mybir.dt.bfloat16
I32 = mybir.dt.int32
AF = mybir.ActivationFunctionType

D = 1024
L = 2048
NT = L // 128
NMEM = 256
EPS = 1e-6
INC = 5632
DFF = 2816
ROPE_THETA = 500000.0
LAMBDA_INIT = 0.8 - 0.6 * math.exp(-0.3 * 0)
N_CORES = 8
TB = 512
STOP_AT = 0
K128 = True


class Reg:
    __slots__ = ("w", "rs")

    def __init__(self):
        self.w = {}
        self.rs = {}


class Prog:
    ENG = ("pe", "act", "dve", "pool", "sp")

    def __init__(self, nc, es, n_dma_sems=16):
        self.nc = nc
        self.ops = {k: [] for k in self.ENG}
        self.cnt = {k: 0 for k in self.ENG}
        self.sem = {k: es.enter_context(nc.semaphore("s_" + k)) for k in self.ENG}
        self.R = n_dma_sems
        self.dsem = {q: [es.enter_context(nc.semaphore(f"d_{q}{i}")) for i in range(n_dma_sems)]
                     for q in ("sp", "pool")}
        self.dcnt = {"sp": 0, "pool": 0}
        self.dtok = {"sp": [], "pool": []}
        self.semobj = {}
        for k in self.ENG:
            self.semobj[("e", k)] = self.sem[k]
        for q in self.dsem:
            for i, s in enumerate(self.dsem[q]):
                self.semobj[("d", q, i)] = s
        self.final = []
        self.bar = {k: set() for k in self.ENG}

    def _deps(self, reads, writes, pwrites):
        deps = set()
        for r in reads:
            for k, v in r.w.items():
                deps.add((k, v))
        for r in writes:
            for k, v in r.w.items():
                deps.add((k, v))
            for k, v in r.rs.items():
                deps.add((k, v))
        for r in pwrites:
            for k, v in r.rs.items():
                deps.add((k, v))
        return deps

    def _commit(self, tok, reads, writes, pwrites):
        k, v = tok
        for r in reads:
            if r.rs.get(k, 0) < v:
                r.rs[k] = v
        for r in writes:
            r.w = {k: v}
            r.rs = {}
        for r in pwrites:
            if r.w.get(k, 0) < v:
                r.w[k] = v

    def op(self, eng, fn, reads=(), writes=(), pwrites=(), deps=()):
        deps = self._deps(reads, writes, pwrites) | set(deps) | self.bar[eng]
        self.bar[eng] = set()
        self.cnt[eng] += 1
        tok = (("e", eng), self.cnt[eng])
        self.ops[eng].append((deps, fn, self.sem[eng], 1))
        self._commit(tok, reads, writes, pwrites)
        return tok

    def barrier(self):
        toks = set()
        for k in self.ENG:
            if self.cnt[k] > 0:
                toks.add((("e", k), self.cnt[k]))
        for q in self.dtok:
            for t in self.dtok[q][-self.R:]:
                toks.add(t)
        self.bar = {k: set(toks) for k in self.ENG}

    def dma(self, q, out, in_, reads=(), writes=(), pwrites=(), final=False, **kw):
        deps = self._deps(reads, writes, pwrites) | self.bar[q]
        self.bar[q] = set()
        i = self.dcnt[q]
        self.dcnt[q] += 1
        slot = i % self.R
        val = 16 * (i // self.R + 1)
        if i >= self.R:
            deps.add(self.dtok[q][i - self.R])
        tok = (("d", q, slot), val)
        self.dtok[q].append(tok)
        fn = lambda e, out=out, in_=in_, kw=kw: e.dma_start(out=out, in_=in_, **kw)
        self.ops[q].append((deps, fn, self.dsem[q][slot], 16))
        self._commit(tok, reads, writes, pwrites)
        if final:
            self.final.append(tok)
        return tok

    def emit(self):
        nc = self.nc
        with nc.Block() as block:
            def run(engname):
                def body(e):
                    waited = {}
                    for deps, fn, sem, amt in self.ops[engname]:
                        for (k, v) in sorted(deps, key=lambda t: str(t)):
                            if engname == "pe" and k == ("e", "pe"):
                                continue
                            if waited.get(k, 0) >= v:
                                continue
                            e.wait_ge(self.semobj[k], v)
                            waited[k] = v
                        fn(e).then_inc(sem, amt)
                    if engname == "sp":
                        for (k, v) in self.final:
                            if waited.get(k, 0) < v:
                                e.wait_ge(self.semobj[k], v)
                                waited[k] = v
                return body
            block.sync(run("sp"))
            block.tensor(run("pe"))
            block.scalar(run("act"))
            block.vector(run("dve"))
            block.gpsimd(run("pool"))


def build_nc(NSEQ, stages="ABC", debug=False):
    nc = bass.Bass("TRN2", target_bir_lowering=False)
    T = NSEQ * L

    def din(name, shape):
        return nc.dram_tensor(name, list(shape), F32, kind="ExternalInput").ap()

    x_d = din("x", [T, D])
    mem_d = din("mem", [NSEQ * NMEM, D])
    g_mix = din("norm_mix", [1, D])
    g_mem = din("norm_mem", [1, D])
    g_ffn = din("norm_ffn", [1, D])
    g_fin = din("norm_final", [1, D])
    w_in = din("w_in", [D, INC])
    lam_re = din("s5_lambda_re", [1, 2, 32, 64])
    lam_im = din("s5_lambda_im", [1, 2, 32, 64])
    log_step = din("s5_log_step", [1, 2, 32])
    b_re = din("s5_b_re", [1, 2, 32, 64, 16])
    b_im = din("s5_b_im", [1, 2, 32, 64, 16])
    c_re = din("s5_c_re", [1, 2, 32, 16, 64])
    c_im = din("s5_c_im", [1, 2, 32, 16, 64])
    s5_d = din("s5_d", [1, 512])
    w_glu = din("s5_w_glu", [512, 512])
    b_glu = din("s5_b_glu", [1, 512])
    lq1 = din("diff_lambda_q1", [1, 64])
    lk1 = din("diff_lambda_k1", [1, 64])
    lq2 = din("diff_lambda_q2", [1, 64])
    lk2 = din("diff_lambda_k2", [1, 64])
    subln = din("diff_subln", [1, 128])
    w_kv = din("w_mem_kv", [D, D])
    w_up = din("w_up", [3 * 512, D])
    w_out = din("w_out", [D, D])
    w_f1 = din("w_ffn_in", [D, 2 * DFF])
    w_f2 = din("w_ffn_out", [DFF, D])
    cmask = din("cmask", [128, 4])
    y_d = nc.dram_tensor("y", [T, D], F32, kind="ExternalOutput").ap()

    def dscr(name, shape, dt):
        kind = "ExternalOutput" if debug else "Internal"
        return nc.dram_tensor(name, list(shape), dt, kind=kind).ap()

    wb_in = nc.dram_tensor("wb_in", [D, INC], BF16, kind="Internal").ap()
    wb_kv = nc.dram_tensor("wb_kv", [D, D], BF16, kind="Internal").ap()
    wb_up = nc.dram_tensor("wb_up", [3 * 512, D], BF16, kind="Internal").ap()
    wb_out = nc.dram_tensor("wb_out", [D, D], BF16, kind="Internal").ap()
    wb_f1 = nc.dram_tensor("wb_f1", [D, 2 * DFF], BF16, kind="Internal").ap()
    wb_f2 = nc.dram_tensor("wb_f2", [DFF, D], BF16, kind="Internal").ap()
    wb_glu = nc.dram_tensor("wb_glu", [512, 512], BF16, kind="Internal").ap()
    ct2_d = nc.dram_tensor("ct2_scr", [3, 128, 2048], BF16, kind="Internal").ap()
    ys5_d = dscr("ys5_scr", [NSEQ * 512, L], BF16)
    x1_d = dscr("x1_scr", [T, D], F32)

    with ExitStack() as es:
        P = Prog(nc, es)

        def sb(name, shape, dt=F32):
            return es.enter_context(nc.sbuf_tensor(name, list(shape), dt))

        banks = [es.enter_context(nc.psum_tensor(f"pb{i}", [128, 512], F32)) for i in range(6)]
        rbank = [Reg() for _ in range(6)]
        tbanks = [es.enter_context(nc.psum_tensor(f"tb{i}", [128, 1024], BF16)) for i in range(2)]
        rtb = [Reg(), Reg()]
        ring = [0]

        def nb():
            i = ring[0] % 6
            ring[0] += 1
            return i

        tring = [0]

        def ntb():
            i = tring[0] % 2
            tring[0] += 1
            return i

        ident = sb("ident", [128, 128], BF16)
        identf = sb("identf", [128, 128], F32)
        ones = sb("ones", [128, 128], BF16)
        r_const = Reg()
        P.op("pool", lambda e: e.iota(identf[:], pattern=[[1, 128]], base=0, channel_multiplier=-1,
                                      allow_small_or_imprecise_dtypes=True), writes=[r_const])
        P.op("dve", lambda e: e.tensor_scalar(out=ident[:], in0=identf[:], scalar1=0.0, scalar2=None,
                                              op0=ALU.is_equal), reads=[r_const], writes=[r_const])
        P.op("dve", lambda e: e.memset(ones[:], 1.0), writes=[r_const])

        gains = sb("gains", [128, 4, D], BF16)
        r_gain = Reg()

        r_w = {}

        def cast_w(name, src, dst, rows, cols):
            cstep = 1408 if cols % 1408 == 0 else (1024 if cols % 1024 == 0 else cols)
            chunks = []
            for c0 in range(0, cols, cstep):
                r = Reg()
                chunks.append((c0, c0 + cstep, r))
                for r0 in range(0, rows, 128):
                    P.dma("pool", dst[r0:r0 + 128, c0:c0 + cstep], src[r0:r0 + 128, c0:c0 + cstep], pwrites=[r])
            r_w[name] = chunks

        def wregs(name, c0, ncols):
            return [r for (a, b, r) in r_w[name] if a < c0 + ncols and b > c0]

        if "A" in stages:
            cast_w("glu", w_glu, wb_glu, 512, 512)
        cast_w("in", w_in, wb_in, D, INC)
        if "B" in stages:
            cast_w("kv", w_kv, wb_kv, D, D)
            cast_w("up", w_up, wb_up, 3 * 512, D)
            cast_w("out", w_out, wb_out, D, D)
        if "C" in stages:
            cast_w("f1", w_f1, wb_f1, D, 2 * DFF)
            cast_w("f2", w_f2, wb_f2, DFF, D)

        def mk_wpool(slot_aps):
            regs = [Reg() for _ in slot_aps]
            ringc = [0]

            def wload(name, wb, row0, nkc, c0, ncols=512):
                i = ringc[0] % len(slot_aps)
                ringc[0] += 1
                src = wb[row0:row0 + nkc * 128, c0:c0 + ncols].rearrange("(kc p) c -> p kc c", p=128)
                P.dma("sp", slot_aps[i][:, 0:nkc, 0:ncols], src, reads=wregs(name, c0, ncols), writes=[regs[i]])
                return slot_aps[i], regs[i]
            return wload

        base_slots = [sb(f"ws{i}", [128, 8, 512], BF16)[:] for i in range(2)]
        wload = mk_wpool(base_slots)

        xin = [sb(f"xin{i}", [128, D], F32) for i in range(2)]
        r_xin = [Reg(), Reg()]
        xnb = [sb(f"xnb{i}", [128, D], BF16) for i in range(2)]
        r_xnb = [Reg(), Reg()]
        stat = sb("stat", [128, 8], F32)
        r_stat = Reg()
        junk, r_junk = None, None
        xring = [0]
        for i, g in enumerate((g_mix, g_mem, g_ffn, g_fin)):
            P.dma("sp", xin[0][:].rearrange("p (o d) -> p o d", o=1), g.partition_broadcast(128), writes=[r_xin[0]])
            P.op("dve", lambda e, i=i: e.tensor_copy(out=gains[:, i, :], in_=xin[0][:]), reads=[r_xin[0]], pwrites=[r_gain])


        def rms_rows(src_tile, r_src, gain_idx, out_bf, r_out):
            P.op("act", lambda e: e.activation(out=out_bf[:], in_=src_tile[:], func=AF.Square,
                                               accum_out=stat[:, 0:1]), reads=[r_src], writes=[r_out, r_stat])
            P.op("act", lambda e: e.activation(out=stat[:, 1:2], in_=stat[:, 0:1], func=AF.Sqrt,
                                               scale=1.0 / D, bias=EPS), reads=[r_stat], writes=[r_stat])
            P.op("dve", lambda e: e.reciprocal(out=stat[:, 2:3], in_=stat[:, 1:2]), reads=[r_stat], writes=[r_stat])
            P.op("dve", lambda e: e.scalar_tensor_tensor(out=out_bf[:], in0=src_tile[:], scalar=stat[:, 2:3],
                                                         in1=gains[:, gain_idx, :], op0=ALU.mult, op1=ALU.mult),
                 reads=[r_src, r_stat, r_gain], writes=[r_out])

        def transpose_rows(src_bf, r_src, ncol, dst_fn, r_dst, evac="dve", extra_w=()):
            nch = ncol // 128
            for c0 in range(0, nch, 8):
                n = min(8, nch - c0)
                bi = ntb()
                tbk = tbanks[bi]
                for j in range(n):
                    P.op("pe", lambda e, j=j, c0=c0, tbk=tbk: e.transpose(
                        tbk[:, j * 128:(j + 1) * 128],
                        src_bf[:, (c0 + j) * 128:(c0 + j + 1) * 128], ident[:]),
                        reads=[r_src, r_const], writes=([rtb[bi]] if j == 0 else []), pwrites=([] if j == 0 else [rtb[bi]]))
                dsts = dst_fn(c0, n)
                if not isinstance(dsts, list):
                    dsts = [(slice(0, 128), dsts)]
                for psl, dst in dsts:
                    src_ps = tbk[psl, 0:n * 128].rearrange("p (n c) -> p n c", c=128)
                    if evac == "dve":
                        P.op("dve", lambda e, dst=dst, src_ps=src_ps: e.tensor_copy(out=dst, in_=src_ps),
                             reads=[rtb[bi]], pwrites=[r_dst] + list(extra_w))
                    else:
                        P.op("act", lambda e, dst=dst, src_ps=src_ps: e.activation(out=dst, in_=src_ps, func=AF.Copy),
                             reads=[rtb[bi]], pwrites=[r_dst] + list(extra_w))

        def load_norm_T(row0, gain_idx, dstT, r_dstT, tok0, src_d=None, extra_w=()):
            src_d = x_d if src_d is None else src_d
            i = xring[0] % 2
            xring[0] += 1
            P.dma("sp", xin[i][:], src_d[row0:row0 + 128, :], writes=[r_xin[i]])
            rms_rows(xin[i], r_xin[i], gain_idx, xnb[i], r_xnb[i])
            transpose_rows(xnb[i], r_xnb[i], D, lambda c0, n: dstT[:, c0:c0 + n, tok0:tok0 + 128], r_dstT, extra_w=extra_w)
            return i

        shared = {}
        if "B" in stages:
            build_stage_b(locals())
        if "C" in stages:
            build_stage_c(locals())
        if "A" in stages:
            build_stage_a(locals())
        else:
            for seq in range(NSEQ):
                if "B" in stages:
                    shared["run_b_seq"](seq, False)
                if "C" in stages:
                    shared["run_c_seq"](seq)
        P.emit()
    return nc


def build_stage_c(env):
    P, nc, sb = env["P"], env["nc"], env["sb"]
    NSEQ, stages = env["NSEQ"], env["stages"]
    banks, rbank, nb = env["banks"], env["rbank"], env["nb"]
    wload, wb_f1, wb_f2 = env["wload"], env["wb_f1"], env["wb_f2"]
    x1_d, y_d, x_d = env["x1_d"], env["y_d"], env["x_d"]
    gains, r_gain, stat, r_stat, junk, r_junk = (env[k] for k in ("gains", "r_gain", "stat", "r_stat", "junk", "r_junk"))
    rms_rows, transpose_rows = env["rms_rows"], env["transpose_rows"]
    src_d = x1_d if "B" in stages else x_d
    NTT = TB // 128
    sh_ = env["shared"].get("c", None)
    if sh_ is None:
        x1t = [sb(f"c_x1_{i}", [128, D], F32) for i in range(NTT)]
        r_x1t = [Reg() for _ in range(NTT)]
        xn2T = sb("c_xn2T", [128, 8, TB], BF16)
        r_xn2T = Reg()
        sg = [sb(f"c_sg{i}", [128, TB], F32) for i in range(2)]
        r_sg = [Reg(), Reg()]
    else:
        x1t, r_x1t, xn2T, r_xn2T, sg, r_sg = (sh_[k] for k in ("x1t", "r_x1t", "xn2T", "r_xn2T", "sg", "r_sg"))
    xn2b, r_xn2b = env["xnb"][0], env["r_xnb"][0]
    aT = sb("c_aT", [128, DFF // 128, TB], BF16)
    r_aT = Reg()
    env["shared"].update(aT=aT, r_aT=r_aT)
    yt, r_yt = env["xin"], env["r_xin"]
    cst = {"sgi": 0, "yi": 0}
    SHc = env["shared"]
    if "xnT" in SHc:
        xa = SHc["xnT"]
        wload = env["mk_wpool"]([xa[:, :, 512 * i:512 * (i + 1)] for i in range(4)] + env["base_slots"])

    x1sets = [(x1t, r_x1t)]
    xnsets = [(xn2T, r_xn2T)]
    if "vflat" in SHc:
        vf32 = SHc["vflat"][:].bitcast(F32)
        x1sets.append(([vf32[:, i * D:(i + 1) * D] for i in range(NTT)], [Reg() for _ in range(NTT)]))
        kflat = SHc["kT"][:].rearrange("p a b -> p (a b)")
        xnsets.append((kflat[:, 0:8 * TB].rearrange("p (k t) -> p k t", t=TB), Reg()))
    else:
        x1sets.append(x1sets[0])
        xnsets.append(xnsets[0])

    def run_seq(seq):
        P.barrier()
        blks = list(range(seq * (L // TB), (seq + 1) * (L // TB)))
        head(blks[0])
        for i, blk in enumerate(blks):
            ffn_in(blk)
            if i + 1 < len(blks):
                head(blks[i + 1])
            ffn_out(blk)

    env["shared"]["run_c_seq"] = run_seq

    def head(blk):
        x1t_, r_x1t_ = x1sets[blk % 2]
        xn2T_, r_xn2T_ = xnsets[blk % 2]
        r_x1d = env["shared"].get("r_x1d", None)
        t0 = blk * TB
        for tt in range(NTT):
            rd = [r_x1d[t0 // L]] if r_x1d is not None else []
            P.dma("sp", x1t_[tt][:], src_d[t0 + tt * 128: t0 + (tt + 1) * 128, :], reads=rd, writes=[r_x1t_[tt]])
            rms_rows(x1t_[tt], r_x1t_[tt], 2, xn2b, r_xn2b)
            transpose_rows(xn2b, r_xn2b, D, lambda c0, n, tt=tt: xn2T_[:, c0:c0 + n, tt * 128:(tt + 1) * 128], r_xn2T_)

    def ffn_in(blk):
        xn2T_, r_xn2T_ = xnsets[blk % 2]
        sgi = cst["sgi"]
        for j4 in range(0, DFF // 128, 4):
            nj = min(4, DFF // 128 - j4)
            wg, rwg = wload("f1", wb_f1, 0, 8, j4 * 128, nj * 128)
            wu, rwu = wload("f1", wb_f1, 0, 8, DFF + j4 * 128, nj * 128)
            for jj in range(nj):
                j = j4 + jj
                bg, bu = nb(), nb()
                for kc in range(8):
                    P.op("pe", lambda e, kc=kc, jj=jj, bg=bg, wg=wg: e.matmul(
                        banks[bg][:, 0:TB], lhsT=wg[:, kc, jj * 128:(jj + 1) * 128], rhs=xn2T_[:, kc, :],
                        start=(kc == 0), stop=(kc == 7)), reads=[rwg, r_xn2T_], writes=[rbank[bg]])
                for kc in range(8):
                    P.op("pe", lambda e, kc=kc, jj=jj, bu=bu, wu=wu: e.matmul(
                        banks[bu][:, 0:TB], lhsT=wu[:, kc, jj * 128:(jj + 1) * 128], rhs=xn2T_[:, kc, :],
                        start=(kc == 0), stop=(kc == 7)), reads=[rwu, r_xn2T_], writes=[rbank[bu]])
                s = sgi % 2
                sgi += 1
                P.op("act", lambda e, s=s, bg=bg: e.activation(out=sg[s][:], in_=banks[bg][:, 0:TB], func=AF.Silu),
                     reads=[rbank[bg]], writes=[r_sg[s]])
                P.op("dve", lambda e, s=s, bu=bu, j=j: e.tensor_tensor(out=aT[:, j, :], in0=banks[bu][:, 0:TB],
                                                                        in1=sg[s][:], op=ALU.mult),
                     reads=[rbank[bu], r_sg[s]], pwrites=[r_aT])
        cst["sgi"] = sgi

    def ffn_out(blk):
        x1t_, r_x1t_ = x1sets[blk % 2]
        yi = cst["yi"]
        t0 = blk * TB
        kgroups = [(0, 8), (8, 8), (16, 6)]
        for half in range(2):
            bos = [nb() for _ in range(NTT)]
            for gi, (k0, nk) in enumerate(kgroups):
                wt, rwt = wload("f2", wb_f2, k0 * 128, nk, half * 512, 512)
                for tt in range(NTT):
                    bo = bos[tt]
                    for kk in range(nk):
                        first = (gi == 0 and kk == 0)
                        last = (gi == 2 and kk == nk - 1)
                        P.op("pe", lambda e, kk=kk, k0=k0, wt=wt, bo=bo, tt=tt, first=first, last=last: e.matmul(
                            banks[bo][:, :], lhsT=aT[:, k0 + kk, tt * 128:(tt + 1) * 128], rhs=wt[:, kk, :],
                            start=first, stop=last), reads=[rwt, r_aT], writes=[rbank[bo]])
            for tt in range(NTT):
                bo = bos[tt]
                P.op("dve", lambda e, tt=tt, bo=bo, half=half: e.tensor_tensor(
                    out=x1t_[tt][:, half * 512:(half + 1) * 512], in0=banks[bo][:, :],
                    in1=x1t_[tt][:, half * 512:(half + 1) * 512], op=ALU.add),
                    reads=[rbank[bo], r_x1t_[tt]], writes=[r_x1t_[tt]])
        for tt in range(NTT):
            y = yi % 2
            yi += 1
            P.op("act", lambda e, tt=tt, y=y: e.activation(out=yt[y][:], in_=x1t_[tt][:], func=AF.Square,
                                                      accum_out=stat[:, 4:5]), reads=[r_x1t_[tt]], writes=[r_yt[y], r_stat])
            P.op("act", lambda e: e.activation(out=stat[:, 5:6], in_=stat[:, 4:5], func=AF.Sqrt,
                                               scale=1.0 / D, bias=EPS), reads=[r_stat], writes=[r_stat])
            P.op("dve", lambda e: e.reciprocal(out=stat[:, 6:7], in_=stat[:, 5:6]), reads=[r_stat], writes=[r_stat])
            P.op("dve", lambda e, tt=tt, y=y: e.scalar_tensor_tensor(
                out=yt[y][:], in0=x1t_[tt][:], scalar=stat[:, 6:7], in1=gains[:, 3, :], op0=ALU.mult, op1=ALU.mult),
                reads=[r_x1t_[tt], r_stat, r_gain], writes=[r_yt[y]])
            P.dma("pool", y_d[t0 + tt * 128: t0 + (tt + 1) * 128, :], yt[y][:], reads=[r_yt[y]], final=True)
        cst["yi"] = yi


def _shared_bufs(env):
    SH, sb = env["shared"], env["sb"]
    if "xnT" not in SH:
        SH["xnT"] = sb("b_xnT", [128, 8, L], BF16); SH["r_xnT"] = Reg()
        SH["kT"] = sb("b_kT", [128, 4, L], BF16); SH["r_kT"] = Reg()
        SH["vflat"] = sb("b_vflat", [128, NT * 512], BF16); SH["r_v"] = Reg()
        SH["mb"] = sb("b_mb", [128, 8, TB], BF16); SH["r_mb"] = Reg()
        SH["xrall"] = sb("b_xrall", [128, 4, D])
        SH["xr"] = [SH["xrall"][:, i, :] for i in range(4)]; SH["r_xr"] = [Reg() for _ in range(4)]
    return SH


def _unpack(env):
    class E:
        pass
    e = E()
    e.__dict__.update(env)
    return e


def build_stage_b(env):
    E = _unpack(env)
    P, nc, sb, banks, rbank, nb = E.P, E.nc, E.sb, E.banks, E.rbank, E.nb
    NSEQ, stages = E.NSEQ, E.stages
    wload, transpose_rows, load_norm_T = E.wload, E.transpose_rows, E.load_norm_T
    ones, r_const = E.ones, E.r_const
    X = mybir.AxisListType.X
    r_x1d = [Reg() for _ in range(NSEQ)]
    env["shared"]["r_x1d"] = r_x1d

    lv = sb("b_lv", [128, 4, 64])
    ltmp = sb("b_ltmp", [128, 2, 64])
    lst = sb("b_lst", [128, 8])
    subg = sb("b_subg", [128, 1])
    r_l = Reg()
    for i, a in enumerate((E.lq1, E.lk1, E.lq2, E.lk2)):
        P.dma("sp", lv[:, i:i + 1, :], a.partition_broadcast(128), pwrites=[r_l])
    P.dma("sp", subg[:], E.subln.rearrange("o e -> e o"), pwrites=[r_l])
    P.op("dve", lambda e: e.tensor_tensor(out=ltmp[:, 0, :], in0=lv[:, 0, :], in1=lv[:, 1, :], op=ALU.mult), reads=[r_l], writes=[r_l])
    P.op("dve", lambda e: e.tensor_tensor(out=ltmp[:, 1, :], in0=lv[:, 2, :], in1=lv[:, 3, :], op=ALU.mult), reads=[r_l], writes=[r_l])
    P.op("dve", lambda e: e.tensor_reduce(out=lst[:, 0:2], in_=ltmp[:], axis=X, op=ALU.add), reads=[r_l], writes=[r_l])
    P.op("act", lambda e: e.activation(out=lst[:, 2:4], in_=lst[:, 0:2], func=AF.Exp), reads=[r_l], writes=[r_l])
    P.op("dve", lambda e: e.tensor_tensor(out=lst[:, 4:5], in0=lst[:, 3:4], in1=lst[:, 2:3], op=ALU.subtract), reads=[r_l], writes=[r_l])
    P.op("dve", lambda e: e.tensor_scalar(out=lst[:, 5:6], in0=lst[:, 4:5], scalar1=-LAMBDA_INIT, scalar2=None, op0=ALU.add), reads=[r_l], writes=[r_l])
    P.op("dve", lambda e: e.tensor_scalar(out=subg[:], in0=subg[:], scalar1=1.0 - LAMBDA_INIT, scalar2=None, op0=ALU.mult), reads=[r_l], writes=[r_l])
    nlam = lst[:, 5:6]

    NTt = NT
    tpos = sb("b_tpos", [128, NTt])
    ang = sb("b_ang", [128, NTt, 8])
    angi = sb("b_angi", [128, NTt, 8], I32)
    angf = sb("b_angf", [128, NTt, 8])
    sh = sb("b_sh", [128, NTt, 8])
    sq = sb("b_sq", [128, NTt, 8])
    cosR = sb("b_cosR", [128, NTt, 8])
    sinR = sb("b_sinR", [128, NTt, 8])
    r_rope = Reg()
    P.op("pool", lambda e: e.iota(tpos[:], pattern=[[128, NTt]], base=0, channel_multiplier=1,
                                  allow_small_or_imprecise_dtypes=True), writes=[r_rope])
    for j in range(8):
        inv = 1.0 / (ROPE_THETA ** (2.0 * j / 16.0)) / (2.0 * math.pi)
        P.op("dve", lambda e, j=j, inv=inv: e.tensor_scalar(out=ang[:, :, j], in0=tpos[:], scalar1=inv, scalar2=None, op0=ALU.mult),
             reads=[r_rope], writes=[r_rope])
    P.op("dve", lambda e: e.tensor_copy(out=angi[:], in_=ang[:]), reads=[r_rope], writes=[r_rope])
    P.op("dve", lambda e: e.tensor_copy(out=angf[:], in_=angi[:]), reads=[r_rope], writes=[r_rope])
    P.op("dve", lambda e: e.tensor_tensor(out=ang[:], in0=ang[:], in1=angf[:], op=ALU.subtract), reads=[r_rope], writes=[r_rope])
    P.op("act", lambda e: e.activation(out=sh[:], in_=ang[:], func=AF.Sin, scale=math.pi), reads=[r_rope], writes=[r_rope])
    P.op("act", lambda e: e.activation(out=sq[:], in_=ang[:], func=AF.Sin, scale=math.pi / 2), reads=[r_rope], writes=[r_rope])
    P.op("dve", lambda e: e.tensor_tensor(out=sq[:], in0=sq[:], in1=sq[:], op=ALU.mult), reads=[r_rope], writes=[r_rope])
    P.op("dve", lambda e: e.tensor_scalar(out=sq[:], in0=sq[:], scalar1=-2.0, scalar2=1.0, op0=ALU.mult, op1=ALU.add), reads=[r_rope], writes=[r_rope])
    P.op("dve", lambda e: e.tensor_tensor(out=angf[:], in0=sh[:], in1=sq[:], op=ALU.mult), reads=[r_rope], writes=[r_rope])
    P.op("dve", lambda e: e.tensor_tensor(out=sh[:], in0=sh[:], in1=sh[:], op=ALU.mult), reads=[r_rope], writes=[r_rope])
    P.op("dve", lambda e: e.tensor_scalar(out=sinR[:], in0=angf[:], scalar1=2.0, scalar2=None, op0=ALU.mult),
         reads=[r_rope], writes=[r_rope])
    P.op("dve", lambda e: e.tensor_scalar(out=cosR[:], in0=sh[:], scalar1=-2.0, scalar2=1.0, op0=ALU.mult, op1=ALU.add),
         reads=[r_rope], writes=[r_rope])

    SH = _shared_bufs(env)
    xnT, r_xnT, kT, r_kT, r_v = SH["xnT"], SH["r_xnT"], SH["kT"], SH["r_kT"], SH["r_v"]
    vv = SH["vflat"][:].rearrange("p (i c) -> p i c", c=512)
    mnT = sb("b_mnT", [128, 8, NMEM], BF16); r_mnT = Reg()
    mkT = sb("b_mkT", [128, 4, NMEM], BF16); r_mkT = Reg()
    mv = sb("b_mv", [128, 2, 512], BF16); r_mv = Reg()
    qf = [sb("b_qf0", [128, 8, 64])[:]]; r_qf = [Reg(), Reg()]
    rt = [sb("b_rt0", [128, 4, 8, 8])[:]]
    qb = [sb("b_qb0", [128, 512], BF16)[:]]; r_qb = [Reg(), Reg()]
    qT = sb("b_qT", [128, 4, TB], BF16); r_qT = Reg()
    qT1 = mnT[:].rearrange("p a b -> p (a b)").rearrange("p (c t) -> p c t", t=TB)
    P.op("pool", lambda e: e.memset(qT[:], 0.0), writes=[r_qT])
    xqT = sb("b_xqT", [128, 4, TB], BF16); r_xqT = Reg()
    pT = [sb(f"b_pT{i}", [128, TB], BF16) for i in range(2)]; r_pT = [Reg() for _ in range(2)]
    at = [sb(f"b_at{i}", [128, TB]) for i in range(3)]; r_at = Reg()
    at.append(at[1])
    osq = sb("b_osq", [128, TB], BF16); r_osq = Reg()
    brA = sb("b_brA", [128, 4, TB], BF16); brB = sb("b_brB", [128, 4, TB], BF16)
    br = [brA, brA, brB]; _rA, _rB = Reg(), Reg(); r_br = [_rA, _rA, _rB]
    r_a = [r_at, Reg(), Reg()]
    sg = [at[0], at[1]]; r_sg = [r_a[0], r_a[1]]
    tm = [at[2]] * 2; r_tm = [r_a[2]] * 2
    mb, r_mb, xr, r_xr = SH["mb"], SH["r_mb"], SH["xr"], SH["r_xr"]
    env["shared"].update(qT=qT, r_qT=r_qT, pT=pT, r_pT=r_pT, at=at, r_at=r_at, brB=brB, brA=brA, xqT=xqT)
    env["shared"]["c"] = dict(x1t=xr, r_x1t=r_xr, xn2T=mb, r_xn2T=r_mb, sg=sg, r_sg=r_sg)
    cnt = {"q": 0, "p": 0, "s": 0}
    r_qT1z = Reg()
    r_atc = [r_at, r_at]

    def proj_rope_T(i, wt, rwt, dst_fn, r_dstT, extra_w=()):
        b = nb()
        for kc in range(8):
            P.op("pe", lambda e, kc=kc, b=b: e.matmul(banks[b][:, :], lhsT=xnT[:, kc, i * 128:(i + 1) * 128], rhs=wt[:, kc, :],
                                                      start=(kc == 0), stop=(kc == 7)), reads=[rwt, r_xnT], writes=[rbank[b]])
        s = cnt["q"] % 2
        cnt["q"] += 1
        q3 = qf[s]
        q3f = q3.rearrange("p c d -> p (c d)")
        P.op("act", lambda e, b=b, q3f=q3f: e.activation(out=q3f, in_=banks[b][:, :], func=AF.Copy),
             reads=[rbank[b]], writes=[r_qf[s]])
        c_ = cosR[:, i:i + 1, :].broadcast_to([128, 8, 8])
        s_ = sinR[:, i:i + 1, :].broadcast_to([128, 8, 8])
        x1, x2 = q3[:, :, 0:8], q3[:, :, 8:16]
        t = rt[s]
        for k, (a, tb_) in enumerate(((x1, c_), (x2, s_), (x2, c_), (x1, s_))):
            P.op("dve", lambda e, k=k, a=a, tb_=tb_, t=t: e.tensor_tensor(out=t[:, k, :, :], in0=a, in1=tb_, op=ALU.mult),
                 reads=[r_qf[s], r_rope], writes=[r_qf[s]])
        P.op("dve", lambda e, t=t, x1=x1: e.tensor_tensor(out=x1, in0=t[:, 0, :, :], in1=t[:, 1, :, :], op=ALU.subtract),
             reads=[r_qf[s]], writes=[r_qf[s]])
        P.op("dve", lambda e, t=t, x2=x2: e.tensor_tensor(out=x2, in0=t[:, 2, :, :], in1=t[:, 3, :, :], op=ALU.add),
             reads=[r_qf[s]], writes=[r_qf[s]])
        qbs = qb[s]
        P.op("dve", lambda e, q3f=q3f, qbs=qbs: e.tensor_copy(out=qbs, in_=q3f),
             reads=[r_qf[s]], writes=[r_qb[s]])
        return lambda: transpose_rows(qbs, r_qb[s], 512, dst_fn, r_dstT, evac="act", extra_w=extra_w)

    def attn(qTh, r_q, kfn, vfn, r_k, r_vv, nkt, scale, comp, first):
        pidx = {}

        def issue_s(kt):
            s = cnt["s"] % 2
            cnt["s"] += 1
            P.op("pe", lambda e, kt=kt, s=s: e.matmul(banks[s][:, :], lhsT=kfn(kt), rhs=qTh, start=True, stop=True),
                 reads=[r_k, r_q], writes=[rbank[s]])
            pTl = [pT[0][:], pT[1][:]] + pTx
            p = cnt["p"] % len(pTl)
            cnt["p"] += 1
            pidx[kt] = p
            pidx[kt] = (p, pTl[p])
            P.op("act", lambda e, s=s, pa=pTl[p]: e.activation(out=pa, in_=banks[s][:, :], func=AF.Exp, scale=scale),
                 reads=[rbank[s]], writes=[r_pTl[p]])

        def issue_pv(kt):
            p, pa = pidx[kt]
            P.op("pe", lambda e, kt=kt, pa=pa: e.matmul(banks[2 + comp][:, :], lhsT=vfn(kt), rhs=pa,
                                                      start=(kt == 0), stop=(kt == nkt - 1)),
                 reads=[r_vv, r_pTl[p]], writes=[rbank[2 + comp]])
            P.op("pe", lambda e, kt=kt, pa=pa: e.matmul(banks[4 + comp][:, :], lhsT=ones[:], rhs=pa,
                                                      start=(kt == 0), stop=(kt == nkt - 1)),
                 reads=[r_const, r_pTl[p]], writes=[rbank[4 + comp]])

        issue_s(0)
        for kt in range(nkt):
            if kt + 1 < nkt:
                issue_s(kt + 1)
            issue_pv(kt)

    bw = {}
    pTx = []
    r_pTl = list(r_pT) + [Reg()]

    def run_seq(seq, have_xnT):
        if "w" not in bw:
            extra = []
            if "aT" in env["shared"]:
                aTv = env["shared"]["aT"]
                extra = [aTv[:, 0:8, :], aTv[:, 8:16, :]]
                tail = aTv[:, 16:22, :].rearrange("p a b -> p (a b)")
                tf = tail[:, 0:2048].bitcast(F32)
                qf.append(tf[:, 0:512].rearrange("p (c d) -> p c d", d=64))
                rt.append(tf[:, 512:768].rearrange("p (k c d) -> p k c d", k=4, c=8))
                qb.append(tail[:, 2048:2560])
                pTx.append(tail[:, 2560:3072])
            else:
                qf.append(qf[0]); rt.append(rt[0]); qb.append(qb[0])
                r_qf[1] = r_qf[0]; r_qb[1] = r_qb[0]
            bw["w"] = env["mk_wpool"](env["base_slots"] + extra)
        wload = bw["w"]
        r_ys5d = env["shared"].get("r_ys5d", None)
        r0 = seq * L
        if not have_xnT:
            for i in range(NT):
                load_norm_T(r0 + i * 128, 0, xnT, r_xnT, i * 128)
        wk, rwk = wload("in", E.wb_in, 0, 8, 1024)
        wv, rwv = wload("in", E.wb_in, 0, 8, 1536)
        pend = None
        for i in range(NT):
            nxt = proj_rope_T(i, wk, rwk, lambda c0, n, i=i: [(slice(0, 128), kT[:, c0:c0 + n, i * 128:(i + 1) * 128])], r_kT)
            b = nb()
            for kc in range(8):
                P.op("pe", lambda e, kc=kc, b=b, i=i, wv=wv: e.matmul(banks[b][:, :], lhsT=xnT[:, kc, i * 128:(i + 1) * 128], rhs=wv[:, kc, :],
                                                               start=(kc == 0), stop=(kc == 7)), reads=[rwv, r_xnT], writes=[rbank[b]])
            P.op("dve", lambda e, b=b, i=i: e.tensor_copy(out=vv[:, i, :], in_=banks[b][:, :]), reads=[rbank[b]], pwrites=[r_v])
            if pend is not None:
                pend()
            pend = nxt
        pend()
        for mt in range(2):
            load_norm_T(seq * NMEM + mt * 128, 1, mnT, r_mnT, mt * 128, src_d=E.mem_d, extra_w=[r_qT])
        wmk, rwmk = wload("kv", E.wb_kv, 0, 8, 0)
        wmv, rwmv = wload("kv", E.wb_kv, 0, 8, 512)
        for h in range(4):
            b = nb()
            for kc in range(8):
                P.op("pe", lambda e, kc=kc, b=b, h=h, wmk=wmk: e.matmul(banks[b][:, 0:NMEM], lhsT=wmk[:, kc, h * 128:(h + 1) * 128], rhs=mnT[:, kc, :],
                                                               start=(kc == 0), stop=(kc == 7)), reads=[rwmk, r_mnT], writes=[rbank[b]])
            P.op("dve", lambda e, b=b, h=h: e.tensor_copy(out=mkT[:, h, :], in_=banks[b][:, 0:NMEM]), reads=[rbank[b]], pwrites=[r_mkT])
        for mt in range(2):
            b = nb()
            for kc in range(8):
                P.op("pe", lambda e, kc=kc, b=b, mt=mt, wmv=wmv: e.matmul(banks[b][:, :], lhsT=mnT[:, kc, mt * 128:(mt + 1) * 128], rhs=wmv[:, kc, :],
                                                                 start=(kc == 0), stop=(kc == 7)), reads=[rwmv, r_mnT], writes=[rbank[b]])
            P.op("dve", lambda e, b=b, mt=mt: e.tensor_copy(out=mv[:, mt, :], in_=banks[b][:, :]), reads=[rbank[b]], pwrites=[r_mv])

        P.op("pool", lambda e: e.memset(qT1[0:64, :, :], 0.0), writes=[r_mnT], pwrites=[r_qT])
        P.op("pool", lambda e: e.memset(qT[64:128, :, :], 0.0), pwrites=[r_qT])
        for blk in range(L // TB):
            t0 = blk * TB
            wq, rwq = wload("in", E.wb_in, 0, 8, 512)
            pend = None
            for tt in range(TB // 128):
                nxt = proj_rope_T(blk * (TB // 128) + tt, wq, rwq, lambda c0, n, tt=tt: [(slice(0, 64), qT[0:64, c0:c0 + n, tt * 128:(tt + 1) * 128]), (slice(64, 128), qT1[64:128, c0:c0 + n, tt * 128:(tt + 1) * 128])], r_qT, extra_w=[r_mnT])
                if pend is not None:
                    pend()
                pend = nxt
            wxq, rwxq = wload("in", E.wb_in, 0, 8, 2048)
            for c in range(4):
                b = nb()
                for kc in range(8):
                    P.op("pe", lambda e, kc=kc, b=b, c=c, wxq=wxq, t0=t0: e.matmul(banks[b][:, :], lhsT=wxq[:, kc, c * 128:(c + 1) * 128], rhs=xnT[:, kc, t0:t0 + TB],
                                                                   start=(kc == 0), stop=(kc == 7)), reads=[rwxq, r_xnT], writes=[rbank[b]])
                P.op("dve", lambda e, b=b, c=c: e.tensor_copy(out=xqT[:, c, :], in_=banks[b][:, :]), reads=[rbank[b]], pwrites=[r_xqT])
            pend()
            pending = [None]

            def flush():
                if pending[0] is not None:
                    pending[0]()
                    pending[0] = None

            def diff_tail(h):
                def run():
                    s = cnt["s"] % 2
                    cnt["s"] += 1
                    P.op("pe", lambda e, s=s: e.matmul(banks[s][:, :], lhsT=ones[:], rhs=osq[:], start=True, stop=True),
                         reads=[r_const, r_osq], writes=[rbank[s]])
                    P.op("act", lambda e, s=s: e.activation(out=at[2][:], in_=banks[s][:, :], func=AF.Ln, scale=1.0 / 128, bias=EPS),
                         reads=[rbank[s]], writes=[r_a[2]])
                    P.op("act", lambda e: e.activation(out=at[2][:], in_=at[2][:], func=AF.Exp, scale=-0.5), reads=[r_a[2]], writes=[r_a[2]])
                    P.op("dve", lambda e, h=h: e.scalar_tensor_tensor(out=br[1][:, h, :], in0=at[1][:], scalar=subg[:, 0:1], in1=at[2][:],
                                                                      op0=ALU.mult, op1=ALU.mult), reads=[r_a[1], r_a[2], r_l], pwrites=[r_br[1]])
                return run

            for h in range(4):
                for comp in range(2):
                    lo = 64 * comp
                    qsrc = (qT if comp == 0 else qT1)
                    attn((qsrc[:, h, :] if K128 else qsrc[lo:lo + 64, h, :]), r_qT,
                         (lambda kt, h=h: kT[:, h, kt * 128:(kt + 1) * 128]) if K128 else (lambda kt, h=h, lo=lo: kT[lo:lo + 64, h, kt * 128:(kt + 1) * 128]),
                         lambda kt, h=h: vv[:, kt, h * 128:(h + 1) * 128], r_kT, r_v, NT, 0.125, comp, True)
                    P.op("dve", lambda e, comp=comp: e.reciprocal(out=at[comp][:], in_=banks[4 + comp][:, :]), reads=[rbank[4 + comp]], writes=[r_a[comp]])
                    P.op("dve", lambda e, comp=comp: e.tensor_tensor(out=at[comp][:], in0=banks[2 + comp][:, :], in1=at[comp][:], op=ALU.mult),
                         reads=[rbank[2 + comp], r_a[comp]], writes=[r_a[comp]])
                    if comp == 0:
                        flush()
                P.op("dve", lambda e: e.scalar_tensor_tensor(out=at[1][:], in0=at[1][:], scalar=nlam, in1=at[0][:], op0=ALU.mult, op1=ALU.add),
                     reads=[r_a[0], r_a[1], r_l], writes=[r_a[1]])
                P.op("dve", lambda e: e.tensor_tensor(out=osq[:], in0=at[1][:], in1=at[1][:], op=ALU.mult), reads=[r_a[1]], writes=[r_osq])
                pending[0] = diff_tail(h)
            for h in range(4):
                attn(xqT[:, h, :], r_xqT,
                     lambda kt, h=h: mkT[:, h, kt * 128:(kt + 1) * 128],
                     lambda kt, h=h: mv[:, kt, h * 128:(h + 1) * 128], r_mkT, r_mv, 2, 128 ** -0.5, 0, True)
                P.op("dve", lambda e: e.reciprocal(out=at[0][:], in_=banks[4][:, :]), reads=[rbank[4]], writes=[r_a[0]])
                P.op("dve", lambda e, h=h: e.tensor_tensor(out=br[2][:, h, :], in0=banks[2][:, :], in1=at[0][:], op=ALU.mult),
                     reads=[rbank[2], r_a[0]], pwrites=[r_br[2]])
                flush()
            def merge(n, first, t0=t0):
                for half in range(2):
                    wu, rwu = wload("up", E.wb_up, n * 512, 4, half * 512)
                    wg, rwg = wload("in", E.wb_in, 0, 8, 2560 + n * 1024 + half * 512)
                    for m4 in range(4):
                        mc = half * 4 + m4
                        bu, bg = nb(), nb()
                        for kc in range(4):
                            P.op("pe", lambda e, kc=kc, bu=bu, m4=m4, n=n, wu=wu: e.matmul(banks[bu][:, :], lhsT=wu[:, kc, m4 * 128:(m4 + 1) * 128],
                                                                                           rhs=br[n][:, kc, :], start=(kc == 0), stop=(kc == 3)),
                                 reads=[rwu, r_br[n]], writes=[rbank[bu]])
                        for kc in range(8):
                            P.op("pe", lambda e, kc=kc, bg=bg, m4=m4, wg=wg, t0=t0: e.matmul(banks[bg][:, :], lhsT=wg[:, kc, m4 * 128:(m4 + 1) * 128],
                                                                                      rhs=xnT[:, kc, t0:t0 + TB], start=(kc == 0), stop=(kc == 7)),
                                 reads=[rwg, r_xnT], writes=[rbank[bg]])
                        s = cnt["q"] % 2
                        cnt["q"] += 1
                        P.op("act", lambda e, s=s, bg=bg: e.activation(out=sg[s][:], in_=banks[bg][:, :], func=AF.Sigmoid),
                             reads=[rbank[bg]], writes=[r_sg[s]])
                        if first:
                            P.op("dve", lambda e, s=s, bu=bu, mc=mc: e.tensor_tensor(out=mb[:, mc, :], in0=banks[bu][:, :], in1=sg[s][:], op=ALU.mult),
                                 reads=[rbank[bu], r_sg[s]], pwrites=[r_mb])
                        else:
                            P.op("dve", lambda e, s=s, bu=bu: e.tensor_tensor(out=tm[s][:], in0=banks[bu][:, :], in1=sg[s][:], op=ALU.mult),
                                 reads=[rbank[bu], r_sg[s]], writes=[r_tm[s]])
                            P.op("dve", lambda e, s=s, mc=mc: e.tensor_tensor(out=mb[:, mc, :], in0=mb[:, mc, :], in1=tm[s][:], op=ALU.add),
                                 reads=[r_tm[s], r_mb], pwrites=[r_mb])
            merge(1, True)
            if r_ys5d is not None:
                src = E.ys5_d[seq * 512:(seq + 1) * 512, t0:t0 + TB].rearrange("(c p) t -> p c t", p=128)
                P.dma("sp", br[0][:], src, reads=[r_ys5d[seq]], writes=[r_br[0]])
            else:
                P.op("pool", lambda e: e.memset(br[0][:], 0.0), writes=[r_br[0]])
            merge(2, False)
            merge(0, False)
            for tt in range(TB // 128):
                P.dma("sp", xr[tt][:], E.x_d[r0 + t0 + tt * 128: r0 + t0 + (tt + 1) * 128, :], writes=[r_xr[tt]])
            for half in range(2):
                wo, rwo = wload("out", E.wb_out, 0, 8, half * 512)
                for tt in range(TB // 128):
                    b = nb()
                    for kc in range(8):
                        P.op("pe", lambda e, kc=kc, b=b, tt=tt, wo=wo: e.matmul(banks[b][:, :], lhsT=mb[:, kc, tt * 128:(tt + 1) * 128], rhs=wo[:, kc, :],
                                                                                start=(kc == 0), stop=(kc == 7)), reads=[rwo, r_mb], writes=[rbank[b]])
                    P.op("dve", lambda e, b=b, tt=tt, half=half: e.tensor_tensor(out=xr[tt][:, half * 512:(half + 1) * 512], in0=banks[b][:, :],
                                                                                 in1=xr[tt][:, half * 512:(half + 1) * 512], op=ALU.add),
                         reads=[rbank[b], r_xr[tt]], writes=[r_xr[tt]])
            for tt in range(TB // 128):
                P.dma("pool", E.x1_d[r0 + t0 + tt * 128: r0 + t0 + (tt + 1) * 128, :], xr[tt][:], reads=[r_xr[tt]], pwrites=[r_x1d[seq]])

    env["shared"]["run_b_seq"] = run_seq


def build_stage_a(env):
    E = _unpack(env)
    P, nc, sb, banks, rbank, nb = E.P, E.nc, E.sb, E.banks, E.rbank, E.nb
    NSEQ = E.NSEQ
    wload, transpose_rows, load_norm_T = E.wload, E.transpose_rows, E.load_norm_T
    tbanks, rtb, ntb, ident, r_const = E.tbanks, E.rtb, E.ntb, E.ident, E.r_const
    SH = env["shared"]
    LOG = 11
    r_ys5d = [Reg() for _ in range(NSEQ)]
    SH["r_ys5d"] = r_ys5d

    _shared_bufs(env)
    xnT, r_xnT, kT, r_kT, vflat, r_v, mb, r_mb, xr, r_xr = (SH[k] for k in ("xnT", "r_xnT", "kT", "r_kT", "vflat", "r_v", "mb", "r_mb", "xr", "r_xr"))
    uT, r_uT = kT, r_kT
    zT, r_zT = vflat[:].rearrange("p (c t) -> p c t", t=L), r_v
    Xb, r_Xb = mb[:].rearrange("p a b -> p (a b)").rearrange("p (c t) -> p c t", t=L), r_mb
    if "aT" in SH:
        aTf = SH["aT"][:].rearrange("p a b -> p (a b)").bitcast(F32)
        r_X = SH["r_aT"]
    else:
        aTf = sb("a_Xf", [128, 5632])[:]
        r_X = Reg()
    Xslots = [aTf[:, 0:2 * L].rearrange("p (c t) -> p c t", t=L),
              SH["xrall"][:].rearrange("p a b -> p (a b)").rearrange("p (c t) -> p c t", t=L)]
    r_Xs = [r_X, Reg()]

    pp = E.xin[1][:, 0:768].rearrange("p (a b) -> p a b", b=32)
    ppi = sb("a_ppi", [128, 32], I32)
    r_pp = E.r_xin[1]
    Are = sb("a_Are", [128, LOG, 32]); Aim = sb("a_Aim", [128, LOG, 32]); Aimn = sb("a_Aimn", [128, LOG, 32])
    r_A = Reg()
    LAMR, LAMI, STP, AR, AI, MAG, YV, KF, FR, SHh, SQq, CH, SN, CS, AM1, NR, NI, DEN, BR, BI, NBI, T0, T1 = range(23)
    col = lambda i: pp[:, i, :]
    P.dma("sp", col(LAMR), E.lam_re[0].rearrange("d (p two) n -> (two n) (d p)", two=2), pwrites=[r_pp], allow_slow_non_contiguous=True)
    P.dma("sp", col(LAMI), E.lam_im[0].rearrange("d (p two) n -> (two n) (d p)", two=2), pwrites=[r_pp], allow_slow_non_contiguous=True)
    lsv = E.log_step[0].rearrange("d (p two) -> two (d p)", two=2)
    for g2 in range(2):
        P.dma("sp", pp[64 * g2:64 * g2 + 64, STP:STP + 1, :], lsv[g2:g2 + 1].partition_broadcast(64), pwrites=[r_pp], allow_slow_non_contiguous=True)

    def dv(fn, eng="dve"):
        P.op(eng, fn, reads=[r_pp], writes=[r_pp])

    TT, TS = "tensor_tensor", "tensor_scalar"
    dv(lambda e: e.activation(out=col(STP), in_=col(STP), func=AF.Exp), "act")
    dv(lambda e: e.tensor_tensor(out=col(AR), in0=col(LAMR), in1=col(STP), op=ALU.mult))
    dv(lambda e: e.tensor_tensor(out=col(AI), in0=col(LAMI), in1=col(STP), op=ALU.mult))
    dv(lambda e: e.activation(out=col(MAG), in_=col(AR), func=AF.Exp), "act")
    dv(lambda e: e.tensor_scalar(out=col(YV), in0=col(AI), scalar1=1.0 / (2 * math.pi), scalar2=None, op0=ALU.mult))
    dv(lambda e: e.tensor_copy(out=ppi[:], in_=col(YV)))
    dv(lambda e: e.tensor_copy(out=col(KF), in_=ppi[:]))
    dv(lambda e: e.tensor_tensor(out=col(FR), in0=col(YV), in1=col(KF), op=ALU.subtract))
    dv(lambda e: e.activation(out=col(SHh), in_=col(FR), func=AF.Sin, scale=math.pi), "act")
    dv(lambda e: e.activation(out=col(SQq), in_=col(FR), func=AF.Sin, scale=math.pi / 2), "act")
    dv(lambda e: e.tensor_tensor(out=col(CH), in0=col(SQq), in1=col(SQq), op=ALU.mult))
    dv(lambda e: e.tensor_scalar(out=col(CH), in0=col(CH), scalar1=-2.0, scalar2=1.0, op0=ALU.mult, op1=ALU.add))
    dv(lambda e: e.tensor_tensor(out=col(SN), in0=col(SHh), in1=col(CH), op=ALU.mult))
    dv(lambda e: e.tensor_scalar(out=col(SN), in0=col(SN), scalar1=2.0, scalar2=None, op0=ALU.mult))
    dv(lambda e: e.tensor_tensor(out=col(CS), in0=col(SHh), in1=col(SHh), op=ALU.mult))
    dv(lambda e: e.tensor_scalar(out=col(CS), in0=col(CS), scalar1=-2.0, scalar2=1.0, op0=ALU.mult, op1=ALU.add))
    P.op("dve", lambda e: e.tensor_tensor(out=Are[:, 0, :], in0=col(MAG), in1=col(CS), op=ALU.mult), reads=[r_pp], writes=[r_A])
    P.op("dve", lambda e: e.tensor_tensor(out=Aim[:, 0, :], in0=col(MAG), in1=col(SN), op=ALU.mult), reads=[r_pp], writes=[r_A])
    for s_ in range(1, LOG):
        P.op("dve", lambda e, s_=s_: e.tensor_tensor(out=col(T0), in0=Are[:, s_ - 1, :], in1=Are[:, s_ - 1, :], op=ALU.mult), reads=[r_A], writes=[r_pp])
        P.op("dve", lambda e, s_=s_: e.tensor_tensor(out=col(T1), in0=Aim[:, s_ - 1, :], in1=Aim[:, s_ - 1, :], op=ALU.mult), reads=[r_A, r_pp], writes=[r_pp])
        P.op("dve", lambda e, s_=s_: e.tensor_tensor(out=Are[:, s_, :], in0=col(T0), in1=col(T1), op=ALU.subtract), reads=[r_pp], writes=[r_A])
        P.op("dve", lambda e, s_=s_: e.tensor_tensor(out=col(T0), in0=Are[:, s_ - 1, :], in1=Aim[:, s_ - 1, :], op=ALU.mult), reads=[r_A, r_pp], writes=[r_pp])
        P.op("dve", lambda e, s_=s_: e.tensor_scalar(out=Aim[:, s_, :], in0=col(T0), scalar1=2.0, scalar2=None, op0=ALU.mult), reads=[r_pp], writes=[r_A])
    P.op("dve", lambda e: e.tensor_scalar(out=Aimn[:], in0=Aim[:], scalar1=-1.0, scalar2=None, op0=ALU.mult), reads=[r_A], writes=[r_A])
    P.op("dve", lambda e: e.tensor_scalar(out=col(AM1), in0=Are[:, 0, :], scalar1=-1.0, scalar2=None, op0=ALU.add), reads=[r_A, r_pp], writes=[r_pp])
    dv(lambda e: e.tensor_tensor(out=col(NR), in0=col(AM1), in1=col(LAMR), op=ALU.mult))
    P.op("dve", lambda e: e.tensor_tensor(out=col(T0), in0=Aim[:, 0, :], in1=col(LAMI), op=ALU.mult), reads=[r_A, r_pp], writes=[r_pp])
    dv(lambda e: e.tensor_tensor(out=col(NR), in0=col(NR), in1=col(T0), op=ALU.add))
    P.op("dve", lambda e: e.tensor_tensor(out=col(NI), in0=Aim[:, 0, :], in1=col(LAMR), op=ALU.mult), reads=[r_A, r_pp], writes=[r_pp])
    dv(lambda e: e.tensor_tensor(out=col(T0), in0=col(AM1), in1=col(LAMI), op=ALU.mult))
    dv(lambda e: e.tensor_tensor(out=col(NI), in0=col(NI), in1=col(T0), op=ALU.subtract))
    dv(lambda e: e.tensor_tensor(out=col(DEN), in0=col(LAMR), in1=col(LAMR), op=ALU.mult))
    dv(lambda e: e.tensor_tensor(out=col(T0), in0=col(LAMI), in1=col(LAMI), op=ALU.mult))
    dv(lambda e: e.tensor_tensor(out=col(DEN), in0=col(DEN), in1=col(T0), op=ALU.add))
    dv(lambda e: e.reciprocal(out=col(DEN), in_=col(DEN)))
    dv(lambda e: e.tensor_tensor(out=col(BR), in0=col(NR), in1=col(DEN), op=ALU.mult))
    dv(lambda e: e.tensor_tensor(out=col(BI), in0=col(NI), in1=col(DEN), op=ALU.mult))
    dv(lambda e: e.tensor_scalar(out=col(NBI), in0=col(BI), scalar1=-1.0, scalar2=None, op0=ALU.mult))

    Bre = xr[0][:].rearrange("p (c k) -> p c k", k=32)
    Bim = xr[1][:].rearrange("p (c k) -> p c k", k=32)
    xnb = E.xnb
    Bb = [xnb[0][:].rearrange("p (c k) -> p c k", k=32), xnb[1][:].rearrange("p (c k) -> p c k", k=32)]
    r_B = Reg()
    P.op("pool", lambda e: e.memset(xr[0][:], 0.0), writes=[r_xr[0]])
    P.op("pool", lambda e: e.memset(xr[1][:], 0.0), writes=[r_xr[1]])
    for g2 in range(2):
        P.dma("sp", Bre[64 * g2:64 * g2 + 64, :, 16 * g2:16 * g2 + 16],
              E.b_re[0].rearrange("d (p two) n h -> two n (d p) h", two=2)[g2], reads=[r_xr[0]], pwrites=[r_B])
        P.dma("sp", Bim[64 * g2:64 * g2 + 64, :, 16 * g2:16 * g2 + 16],
              E.b_im[0].rearrange("d (p two) n h -> two n (d p) h", two=2)[g2], reads=[r_xr[1]], pwrites=[r_B])
    r_Bb = Reg()
    for c in range(32):
        P.op("dve", lambda e, c=c: e.tensor_scalar(out=Bb[0][:, c, :], in0=Bre[:, c, :], scalar1=pp[:, BR, c:c + 1], scalar2=None, op0=ALU.mult),
             reads=[r_B, r_pp, E.r_xnb[0], r_xr[0], r_xr[1]], writes=[r_Bb, E.r_xnb[0]])
        P.op("dve", lambda e, c=c: e.scalar_tensor_tensor(out=Bb[0][:, c, :], in0=Bim[:, c, :], scalar=pp[:, NBI, c:c + 1], in1=Bb[0][:, c, :],
                                                          op0=ALU.mult, op1=ALU.add), reads=[r_B, r_pp, r_xr[0], r_xr[1]], writes=[r_Bb, E.r_xnb[0]])
        P.op("dve", lambda e, c=c: e.tensor_scalar(out=Bb[1][:, c, :], in0=Bre[:, c, :], scalar1=pp[:, BI, c:c + 1], scalar2=None, op0=ALU.mult),
             reads=[r_B, r_pp, E.r_xnb[1], r_xr[0], r_xr[1]], writes=[r_Bb, E.r_xnb[1]])
        P.op("dve", lambda e, c=c: e.scalar_tensor_tensor(out=Bb[1][:, c, :], in0=Bim[:, c, :], scalar=pp[:, BR, c:c + 1], in1=Bb[1][:, c, :],
                                                          op0=ALU.mult, op1=ALU.add), reads=[r_B, r_pp, r_xr[0], r_xr[1]], writes=[r_Bb, E.r_xnb[1]])
    BbT = sb("a_BbT", [128, 16, 128], BF16); r_BbT = Reg()
    CT = sb("a_CT", [128, 16, 128], BF16); r_CT = Reg()
    idx = lambda d_, part, cc: (d_ * 2 + part) * 4 + cc
    for d_ in range(2):
        for part in range(2):
            for cc in range(4):
                bi = ntb()
                c0 = d_ * 16 + 4 * cc
                src = xnb[part][:, c0 * 32:(c0 + 4) * 32]
                P.op("pe", lambda e, bi=bi, src=src: e.transpose(tbanks[bi][:, 0:128], src, ident[:]),
                     reads=[r_Bb, r_const, E.r_xnb[part]], writes=[rtb[bi]])
                P.op("dve", lambda e, bi=bi, k=idx(d_, part, cc): e.tensor_copy(out=BbT[:, k, :], in_=tbanks[bi][:, 0:128]),
                     reads=[rtb[bi]], pwrites=[r_BbT])

    FOLD0 = "at" in SH
    BbT2 = None
    if FOLD0:
        BbT2 = sb("a_BbT2", [128, 16, 128], BF16); r_BbT2 = Reg()
        at_ = SH["at"]
        B2 = [at_[0][:].bitcast(BF16), at_[1][:].bitcast(BF16)]
        B2v = [B2[0].rearrange("p (c k) -> p c k", k=32), B2[1].rearrange("p (c k) -> p c k", k=32)]
        r_B2 = SH["r_at"]
        for c in range(32):
            P.op("dve", lambda e, c=c: e.tensor_scalar(out=B2v[0][:, c, :], in0=Bb[0][:, c, :], scalar1=Are[:, 0, c:c + 1], scalar2=None, op0=ALU.mult),
                 reads=[r_Bb, r_A, E.r_xnb[0], E.r_xnb[1]], writes=[r_B2])
            P.op("dve", lambda e, c=c: e.scalar_tensor_tensor(out=B2v[0][:, c, :], in0=Bb[1][:, c, :], scalar=Aimn[:, 0, c:c + 1], in1=B2v[0][:, c, :],
                                                              op0=ALU.mult, op1=ALU.add), reads=[r_Bb, r_A, E.r_xnb[0], E.r_xnb[1]], writes=[r_B2])
            P.op("dve", lambda e, c=c: e.tensor_scalar(out=B2v[1][:, c, :], in0=Bb[1][:, c, :], scalar1=Are[:, 0, c:c + 1], scalar2=None, op0=ALU.mult),
                 reads=[r_Bb, r_A, E.r_xnb[0], E.r_xnb[1]], writes=[r_B2])
            P.op("dve", lambda e, c=c: e.scalar_tensor_tensor(out=B2v[1][:, c, :], in0=Bb[0][:, c, :], scalar=Aim[:, 0, c:c + 1], in1=B2v[1][:, c, :],
                                                              op0=ALU.mult, op1=ALU.add), reads=[r_Bb, r_A, E.r_xnb[0], E.r_xnb[1]], writes=[r_B2])
        for d_ in range(2):
            for part in range(2):
                for cc in range(4):
                    bi = ntb()
                    c0 = d_ * 16 + 4 * cc
                    src = B2[part][:, c0 * 32:(c0 + 4) * 32]
                    P.op("pe", lambda e, bi=bi, src=src: e.transpose(tbanks[bi][:, 0:128], src, ident[:]),
                         reads=[r_B2, r_const], writes=[rtb[bi]])
                    P.op("dve", lambda e, bi=bi, k=idx(d_, part, cc): e.tensor_copy(out=BbT2[:, k, :], in_=tbanks[bi][:, 0:128]),
                         reads=[rtb[bi]], pwrites=[r_BbT2])

    Cn = [xr[2][:, 0:512].rearrange("p (c n) -> p c n", n=64), xr[3][:, 0:512].rearrange("p (c n) -> p c n", n=64)]
    r_Cn = Reg()
    for part, csrc in enumerate((E.c_re, E.c_im)):
        P.dma("sp", Cn[part], csrc[0].rearrange("d (cc g8) h n -> (g8 h) (d cc) n", g8=8), reads=[r_xr[2 + part]], writes=[r_xr[2 + part]])
    msk = sb("a_msk", [128, 4])
    r_m = Reg()
    P.dma("sp", msk[:], E.cmask, writes=[r_m])
    if "qT" in SH:
        in2 = SH["qT"][:].rearrange("p a b -> p (a b)").rearrange("p (a b c) -> p a b c", a=2, b=8)
        r_in2 = SH["r_qT"]
    else:
        in2 = sb("a_in2", [128, 2, 8, 128], BF16)[:]
        r_in2 = Reg()
    for part in range(2):
        for g2 in range(2):
            mcol = msk[:, 2 * part + g2: 2 * part + g2 + 1]
            P.op("dve", lambda e, part=part, g2=g2, mcol=mcol: e.tensor_scalar(out=in2[:, part, :, 64 * g2:64 * g2 + 64], in0=Cn[part], scalar1=mcol,
                                                                               scalar2=None, op0=ALU.mult),
                 reads=[r_xr[2 + part], r_m], pwrites=[r_in2])
    for d_ in range(2):
        for part in range(2):
            for cc in range(4):
                bi = ntb()
                P.op("pe", lambda e, bi=bi, part=part, k=d_ * 4 + cc: e.transpose(tbanks[bi][:, 0:128], in2[:, part, k, :], ident[:]),
                     reads=[r_in2, r_const], writes=[rtb[bi]])
                P.op("dve", lambda e, bi=bi, k=idx(d_, part, cc): e.tensor_copy(out=CT[:, k, :], in_=tbanks[bi][:, 0:128]),
                     reads=[rtb[bi]], pwrites=[r_CT])
    FOLDD = all(k in SH for k in ("brB", "brA", "xqT"))
    npow = 3 if FOLDD else 1
    if FOLDD:
        ctts = [SH["brB"], SH["brA"], SH["xqT"]]
    elif "brB" in SH:
        ctts = [SH["brB"]]
    else:
        ctts = [sb("a_ct2", [128, 4, TB], BF16)]
    CTP = [t_[:].rearrange("p a b -> p (a b)").rearrange("p (k c) -> p k c", c=128) for t_ in ctts]
    r_CT2 = Reg()
    P.op("dve", lambda e: e.tensor_tensor(out=col(AR), in0=Are[:, 1, :], in1=Are[:, 0, :], op=ALU.mult), reads=[r_A, r_pp], writes=[r_pp])
    P.op("dve", lambda e: e.tensor_tensor(out=col(T0), in0=Aim[:, 1, :], in1=Aim[:, 0, :], op=ALU.mult), reads=[r_A, r_pp], writes=[r_pp])
    P.op("dve", lambda e: e.tensor_tensor(out=col(AR), in0=col(AR), in1=col(T0), op=ALU.subtract), reads=[r_pp], writes=[r_pp])
    P.op("dve", lambda e: e.tensor_tensor(out=col(AI), in0=Are[:, 1, :], in1=Aim[:, 0, :], op=ALU.mult), reads=[r_A, r_pp], writes=[r_pp])
    P.op("dve", lambda e: e.tensor_tensor(out=col(T0), in0=Aim[:, 1, :], in1=Are[:, 0, :], op=ALU.mult), reads=[r_A, r_pp], writes=[r_pp])
    P.op("dve", lambda e: e.tensor_tensor(out=col(AI), in0=col(AI), in1=col(T0), op=ALU.add), reads=[r_pp], writes=[r_pp])
    P.op("dve", lambda e: e.tensor_scalar(out=col(T1), in0=col(AI), scalar1=-1.0, scalar2=None, op0=ALU.mult), reads=[r_pp], writes=[r_pp])
    pw = [(Are[:, 0, :], Aim[:, 0, :], Aimn[:, 0, :]), (Are[:, 1, :], Aim[:, 1, :], Aimn[:, 1, :]), (col(AR), col(AI), col(T1))]
    tmpc = SH["pT"][0][:].rearrange("p (a b c) -> p a b c", a=4, b=4) if "pT" in SH else sb("a_tmpc", [128, 4, 4, 32], BF16)[:]
    r_tmpc = SH["r_pT"][0] if "pT" in SH else Reg()
    for m in range(npow):
        pa, pb, pnb = pw[m]
        C2 = CTP[m]
        for d_ in range(2):
            def v4(t_, part):
                k = idx(d_, part, 0)
                return t_[:, k:k + 4, :].rearrange("p k (q c) -> p k q c", c=32)

            def sc(tab):
                return tab[:, d_ * 16:(d_ + 1) * 16].rearrange("p (k q o) -> p k q o", q=4, o=1).broadcast_to([128, 4, 4, 32])
            c0v, c1v = v4(CT, 0), v4(CT, 1)
            o0, o1 = v4(C2, 0), v4(C2, 1)
            P.op("dve", lambda e, o0=o0, c0v=c0v, a_=sc(pa): e.tensor_tensor(out=o0, in0=c0v, in1=a_, op=ALU.mult), reads=[r_CT, r_A, r_pp], pwrites=[r_CT2])
            P.op("dve", lambda e, c1v=c1v, b_=sc(pb): e.tensor_tensor(out=tmpc, in0=c1v, in1=b_, op=ALU.mult), reads=[r_CT, r_A, r_pp], writes=[r_tmpc])
            P.op("dve", lambda e, o0=o0: e.tensor_tensor(out=o0, in0=o0, in1=tmpc, op=ALU.add), reads=[r_tmpc, r_CT2], pwrites=[r_CT2])
            P.op("dve", lambda e, o1=o1, c1v=c1v, a_=sc(pa): e.tensor_tensor(out=o1, in0=c1v, in1=a_, op=ALU.mult), reads=[r_CT, r_A, r_pp], pwrites=[r_CT2])
            P.op("dve", lambda e, c0v=c0v, b_=sc(pb): e.tensor_tensor(out=tmpc, in0=c0v, in1=b_, op=ALU.mult), reads=[r_CT, r_A, r_pp, r_CT2], writes=[r_tmpc])
            P.op("dve", lambda e, o1=o1: e.tensor_tensor(out=o1, in0=o1, in1=tmpc, op=ALU.subtract), reads=[r_tmpc, r_CT2], pwrites=[r_CT2])
    r_ct2d = Reg()
    for m in range(npow):
        P.dma("sp", E.ct2_d[m], ctts[m][:].rearrange("p a b -> p (a b)"), reads=[r_CT2], pwrites=[r_ct2d])
    CT2 = CTP[0]
    dcol = sb("a_dcol", [128, 8]); r_dc = Reg()
    P.dma("sp", dcol[:, 0:4], E.s5_d[0].rearrange("(c p) -> p c", p=128), pwrites=[r_dc], allow_slow_non_contiguous=True)
    P.dma("sp", dcol[:, 4:8], E.b_glu[0].rearrange("(c p) -> p c", p=128), pwrites=[r_dc], allow_slow_non_contiguous=True)

    gl = [aTf[:, 2 * L + i * TB: 2 * L + (i + 1) * TB] for i in range(3)]; r_gl = [Reg(), Reg(), Reg()]
    if E.debug:
        def dbg(name, ap, shape, dt, reads):
            dd = nc.dram_tensor(name, list(shape), dt, kind="ExternalOutput").ap()
            P.dma("sp", dd, ap, reads=reads, final=True)
        dbg("dbg_Are", Are[:].rearrange("p a b -> p (a b)"), [128, LOG * 32], F32, [r_A])
        dbg("dbg_Aim", Aim[:].rearrange("p a b -> p (a b)"), [128, LOG * 32], F32, [r_A])
        dbg("dbg_pp", E.xin[1][:, 0:768], [128, 768], F32, [r_pp])
        dbg("dbg_BbT", BbT[:].rearrange("p a b -> p (a b)"), [128, 16 * 128], BF16, [r_BbT])
        dbg("dbg_CT", CT[:].rearrange("p a b -> p (a b)"), [128, 16 * 128], BF16, [r_CT])
    if "pT" in SH:
        yst = [SH["pT"][0][:], SH["pT"][1][:]]; r_yst = SH["r_pT"]
    else:
        yst = [sb(f"a_yst{i}", [128, TB], BF16)[:] for i in range(2)]; r_yst = [Reg(), Reg()]
    GC = 2.0 * math.sqrt(2.0 / math.pi)
    cn = {"e": 0, "y": 0, "x": 0}

    def scan(ci, rev, eng, Xs, r_X):
        def sl(part, start, n, S):
            return Xs[:, part, start:start + (n - 1) * S + 1:S]

        st = {"t3": None, "t4": None, "first": True}

        def level(d, dst0, src0, n, S):
            a = Are[:, d, ci:ci + 1]
            b = Aim[:, d, ci:ci + 1]
            nbm = Aimn[:, d, ci:ci + 1]
            rd, idd = sl(0, dst0, n, S), sl(1, dst0, n, S)
            rs, is_ = sl(0, src0, n, S), sl(1, src0, n, S)

            def stt(o, i0, sc, deps, first):
                return P.op(eng, lambda e, o=o, i0=i0, sc=sc: e.scalar_tensor_tensor(out=o, in0=i0, scalar=sc, in1=o, op0=ALU.mult, op1=ALU.add),
                            reads=([r_A, r_X] if first else [r_A]), deps=deps)

            f = st["first"]
            t1 = stt(rd, rs, a, [st["t3"]] if st["t3"] else [], f)
            t2 = stt(idd, is_, a, [st["t4"]] if st["t4"] else [], f)
            st["t3"] = stt(rd, is_, nbm, [t1], False)
            st["t4"] = stt(idd, rs, b, [t2], False)
            st["first"] = False

        for d in range(1 if FOLD0 else 0, LOG):
            S = 2 << d
            h = 1 << d
            n = L // S
            if not rev:
                level(d, S - 1, h - 1, n, S)
            else:
                level(d, 0, h, n, S)
        for d in range(LOG - 2, (1 if FOLDD else 0), -1):
            S = 2 << d
            h = 1 << d
            n = L // S - 1
            if not rev:
                level(d, S + h - 1, S - 1, n, S)
            else:
                level(d, S - h, S, n, S)
        k, v = st["t4"]
        r_X.w = {k: v}
        r_X.rs = {}

    for seq in range(NSEQ):
        P.barrier()
        for m in range(npow):
            P.dma("sp", ctts[m][:].rearrange("p a b -> p (a b)"), E.ct2_d[m], reads=[r_ct2d], pwrites=[r_CT2])
        r0 = seq * L
        for i in range(NT):
            load_norm_T(r0 + i * 128, 0, xnT, r_xnT, i * 128)
        wu, rwu = wload("in", E.wb_in, 0, 8, 0)
        for c in range(4):
            for tb in range(4):
                b = nb()
                for kc in range(8):
                    P.op("pe", lambda e, kc=kc, b=b, c=c, tb=tb, wu=wu: e.matmul(banks[b][:, :], lhsT=wu[:, kc, c * 128:(c + 1) * 128],
                                                                                 rhs=xnT[:, kc, tb * 512:(tb + 1) * 512], start=(kc == 0), stop=(kc == 7)),
                         reads=[rwu, r_xnT], writes=[rbank[b]])
                P.op("act", lambda e, b=b, c=c, tb=tb: e.activation(out=uT[:, c, tb * 512:(tb + 1) * 512], in_=banks[b][:, :], func=AF.Copy),
                     reads=[rbank[b]], pwrites=[r_uT])
        tiles = [(cc, q, d_) for cc in range(4) for q in range(4) for d_ in range(2)]
        slots = {}

        def front(ti):
            cc, q, d_ = tiles[ti]
            xsl = cn["x"] % 2
            cn["x"] += 1
            slots[ti] = xsl
            Xs, r_X = Xslots[xsl], r_Xs[xsl]
            for part in range(2):
                if not FOLD0:
                    groups = [(tb * 512, 1, tb * 512, None) for tb in range(4)]
                else:
                    pd = 1 if d_ == 0 else 0
                    groups = []
                    for jb in range(2):
                        groups.append((jb * 1024 + (1 - pd), 2, jb * 1024 + (1 - pd), None))
                        groups.append((jb * 1024 + pd, 2, jb * 1024 + pd, jb * 1024 + (1 - pd)))
                for (o0, st_, u0, u1) in groups:
                    s = cn["e"] % 2
                    cn["e"] += 1
                    usl = slice(u0, u0 + 511 * st_ + 1, st_)
                    osl = slice(o0, o0 + 511 * st_ + 1, st_)
                    P.op("pe", lambda e, s=s, q=q, k=idx(d_, part, cc), cc=cc, usl=usl, last=(u1 is None): e.matmul(
                        banks[s][:, :], lhsT=BbT[32 * q:32 * q + 32, k, :], rhs=uT[32 * q:32 * q + 32, cc, usl],
                        start=True, stop=last, tile_position=(32 * q, 0)), reads=[r_BbT, r_uT], writes=[rbank[s]])
                    if u1 is not None:
                        usl2 = slice(u1, u1 + 511 * st_ + 1, st_)
                        P.op("pe", lambda e, s=s, q=q, k=idx(d_, part, cc), cc=cc, usl2=usl2: e.matmul(
                            banks[s][:, :], lhsT=BbT2[32 * q:32 * q + 32, k, :], rhs=uT[32 * q:32 * q + 32, cc, usl2],
                            start=False, stop=True, tile_position=(32 * q, 0)), reads=[r_BbT2, r_uT], writes=[rbank[s]])
                    P.op("act", lambda e, s=s, part=part, osl=osl, Xs=Xs: e.activation(out=Xs[:, part, osl], in_=banks[s][:, :], func=AF.Copy),
                         reads=[rbank[s]], pwrites=[r_X])

        def back(ti):
            cc, q, d_ = tiles[ti]
            xsl = slots[ti]
            Xs, r_X = Xslots[xsl], r_Xs[xsl]
            ci = d_ * 16 + 4 * cc + q
            scan(ci, d_ == 1, "dve", Xs, r_X)
            P.op("act", lambda e, Xs=Xs: e.activation(out=Xb, in_=Xs, func=AF.Copy), reads=[r_X], writes=[r_Xb])
            if FOLDD:
                if d_ == 0:
                    extra = {3: [], 1: [(1, -2, 1, 511)], 0: [(0, -1, 1, 511)], 2: [(0, -1, 0, 512), (2, -3, 1, 511)]}
                else:
                    extra = {0: [], 2: [(1, 2, 0, 511)], 3: [(0, 1, 0, 511)], 1: [(0, 1, 0, 512), (2, 3, 0, 511)]}
                for c in range(4):
                    bk = 2 + c
                    for part in range(2):
                        first = (d_ == 0 and part == 0)
                        last = (d_ == 1 and part == 1)
                        ex = extra[c]
                        P.op("pe", lambda e, q=q, k=idx(d_, part, cc), bk=bk, c=c, part=part, first=first, last=(last and not ex): e.matmul(
                            banks[bk][32 * q:32 * q + 32, :], lhsT=CT[:, k, 32 * q:32 * q + 32], rhs=Xb[:, part, c:c + 2045:4],
                            start=first, stop=last, tile_position=(0, 32 * q)), reads=[r_CT, r_Xb], writes=[rbank[bk]])
                        for ei, (m, sh_, j0, ncol) in enumerate(ex):
                            src0 = 4 * j0 + c + sh_
                            P.op("pe", lambda e, q=q, k=idx(d_, part, cc), bk=bk, part=part, m=m, j0=j0, src0=src0, ncol=ncol, last=(last and ei == len(ex) - 1): e.matmul(
                                banks[bk][32 * q:32 * q + 32, j0:j0 + ncol], lhsT=CTP[m][:, k, 32 * q:32 * q + 32],
                                rhs=Xb[:, part, src0:src0 + 4 * (ncol - 1) + 1:4],
                                start=False, stop=last, tile_position=(0, 32 * q)), reads=[r_CT2, r_Xb], writes=[rbank[bk]])
                return
            fpar = 0 if d_ == 0 else 1
            for jb in range(2):
                for par in range(2):
                    bk = 2 + jb * 2 + par
                    base = jb * 1024 + par
                    for part in range(2):
                        first = (d_ == 0 and part == 0)
                        last = (d_ == 1 and part == 1)
                        folded = (par == fpar)
                        P.op("pe", lambda e, q=q, k=idx(d_, part, cc), bk=bk, base=base, part=part, first=first, last=(last and not folded): e.matmul(
                            banks[bk][32 * q:32 * q + 32, :], lhsT=CT[:, k, 32 * q:32 * q + 32], rhs=Xb[:, part, base:base + 1023:2],
                            start=first, stop=last, tile_position=(0, 32 * q)), reads=[r_CT, r_Xb], writes=[rbank[bk]])
                        if folded:
                            if d_ == 0:
                                c0 = 1 if jb == 0 else 0
                                src0 = base + 2 * c0 - 1
                                ncol = 512 - c0
                            else:
                                c0 = 0
                                src0 = base + 1
                                ncol = 512 if jb == 0 else 511
                            P.op("pe", lambda e, q=q, k=idx(d_, part, cc), bk=bk, part=part, c0=c0, src0=src0, ncol=ncol, last=last: e.matmul(
                                banks[bk][32 * q:32 * q + 32, c0:c0 + ncol], lhsT=CT2[:, k, 32 * q:32 * q + 32],
                                rhs=Xb[:, part, src0:src0 + 2 * (ncol - 1) + 1:2],
                                start=False, stop=last, tile_position=(0, 32 * q)), reads=[r_CT2, r_Xb], writes=[rbank[bk]])

        def gelu_chunk(cc):
            for tb in range(4):
                jb_, par_ = tb // 2, tb % 2
                tsl = slice(tb, tb + 2045, 4) if FOLDD else slice(jb_ * 1024 + par_, jb_ * 1024 + par_ + 1023, 2)
                P.op("dve", lambda e, tb=tb, cc=cc, tsl=tsl: e.scalar_tensor_tensor(out=gl[0], in0=uT[:, cc, tsl], scalar=dcol[:, cc:cc + 1], in1=banks[2 + tb][:, :],
                                                                                   op0=ALU.mult, op1=ALU.add), reads=[r_uT, r_dc, rbank[2 + tb]], writes=[r_gl[0]])
                P.op("act", lambda e: e.activation(out=gl[1], in_=gl[0], func=AF.Square), reads=[r_gl[0]], writes=[r_gl[1]])
                P.op("dve", lambda e: e.tensor_scalar(out=gl[1], in0=gl[1], scalar1=0.044715, scalar2=1.0, op0=ALU.mult, op1=ALU.add), reads=[r_gl[1]], writes=[r_gl[1]])
                P.op("dve", lambda e: e.tensor_tensor(out=gl[1], in0=gl[1], in1=gl[0], op=ALU.mult), reads=[r_gl[0], r_gl[1]], writes=[r_gl[1]])
                P.op("act", lambda e: e.activation(out=gl[2], in_=gl[1], func=AF.Sigmoid, scale=GC), reads=[r_gl[1]], writes=[r_gl[2]])
                P.op("dve", lambda e, cc=cc, tsl=tsl: e.tensor_tensor(out=zT[:, cc, tsl], in0=gl[0], in1=gl[2], op=ALU.mult), reads=[r_gl[0], r_gl[2]], pwrites=[r_zT])

        front(0)
        for ti in range(len(tiles)):
            if ti + 1 < len(tiles):
                front(ti + 1)
            back(ti)
            if ti % 8 == 7:
                gelu_chunk(tiles[ti][0])
        wgl, rwgl = wload("glu", E.wb_glu, 0, 4, 0)
        for mc in range(4):
            for tb in range(4):
                b = nb()
                tsl = slice(tb * 512, (tb + 1) * 512)
                for kc in range(4):
                    P.op("pe", lambda e, kc=kc, b=b, mc=mc, tsl=tsl, wgl=wgl: e.matmul(banks[b][:, :], lhsT=wgl[:, kc, mc * 128:(mc + 1) * 128], rhs=zT[:, kc, tsl],
                                                                                       start=(kc == 0), stop=(kc == 3)), reads=[rwgl, r_zT], writes=[rbank[b]])
                P.op("act", lambda e, b=b, mc=mc: e.activation(out=gl[2], in_=banks[b][:, :], func=AF.Sigmoid, bias=dcol[:, 4 + mc:5 + mc]),
                     reads=[rbank[b], r_dc], writes=[r_gl[2]])
                y = cn["y"] % 2
                cn["y"] += 1
                P.op("dve", lambda e, y=y, mc=mc, tsl=tsl: e.tensor_tensor(out=yst[y], in0=zT[:, mc, tsl], in1=gl[2], op=ALU.mult),
                     reads=[r_zT, r_gl[2]], writes=[r_yst[y]])
                P.dma("pool", E.ys5_d[seq * 512 + mc * 128: seq * 512 + (mc + 1) * 128, tb * 512:(tb + 1) * 512], yst[y],
                      reads=[r_yst[y]], pwrites=[r_ys5d[seq]])
        P.barrier()
        if "B" in E.stages:
            SH["run_b_seq"](seq, True)
        if "C" in E.stages:
            SH["run_c_seq"](seq)


_PARAM_KEYS = ["norm_mix", "norm_mem", "norm_ffn", "w_in", "s5_lambda_re", "s5_lambda_im", "s5_log_step",
               "s5_b_re", "s5_b_im", "s5_c_re", "s5_c_im", "s5_d", "s5_w_glu", "s5_b_glu",
               "diff_lambda_q1", "diff_lambda_k1", "diff_lambda_q2", "diff_lambda_k2", "diff_subln",
               "w_mem_kv", "w_up", "w_out", "w_ffn_in", "w_ffn_out"]


def _core_inputs(inputs, core, nseq):
    f = lambda a: np.ascontiguousarray(np.asarray(a, dtype=np.float32))
    x_all = inputs["_x_all"]
    m_all = inputs["_m_all"]
    m = {"x": x_all[core * nseq:(core + 1) * nseq].reshape(nseq * L, D),
         "mem": m_all[core * nseq:(core + 1) * nseq].reshape(nseq * NMEM, D)}
    for k in _PARAM_KEYS:
        a = f(inputs[k])
        if k in ("w_in", "w_mem_kv", "w_out", "w_ffn_in", "w_ffn_out", "s5_w_glu"):
            a = a[0]
        elif k == "w_up":
            a = a[0].reshape(3 * 512, D)
        m[k] = np.ascontiguousarray(a)
    m["norm_final"] = f(inputs["norm_final"]).reshape(1, D)
    par = ((np.arange(128) // 16) % 2).astype(np.float32)
    m["cmask"] = np.stack([1 - par, par, par - 1, -par], axis=1).astype(np.float32)
    return m


def kernel(**inputs):
    xp = np.asarray(inputs["x_prompt"], dtype=np.float32)
    xs = np.asarray(inputs["x_sample"], dtype=np.float32)
    mp = np.asarray(inputs["mem_prompt"], dtype=np.float32)
    ms = np.asarray(inputs["mem_sample"], dtype=np.float32)
    inputs = dict(inputs)
    inputs["_x_all"] = np.concatenate([xp, xs], axis=0)
    inputs["_m_all"] = np.concatenate([mp, ms], axis=0)
    nseq = inputs["_x_all"].shape[0] // N_CORES
    nc = build_nc(nseq)
    in_maps = [_core_inputs(inputs, c, nseq) for c in range(N_CORES)]
    res = run_bass_kernel_spmd(nc, in_maps, core_ids=list(range(N_CORES)))
    y = np.concatenate([r["y"].reshape(nseq, L, D) for r in res.results], axis=0).astype(np.float32)
    return (np.ascontiguousarray(y[:xp.shape[0]]), np.ascontiguousarray(y[xp.shape[0]:]))
```
